# Optimizing a Trainium2 kernel written in Bass

```python
import math
import jax, jax.numpy as jnp
from jax import lax
import numpy as np

D_MODEL = 1024
BATCH = 8
SEQ = 2048
DEPTH = 4
DEC_BATCH = 128
DEC_SEQ = 1
PAST_LEN = 16384
PAGE_SIZE = 128

N_A_LAYERS = (DEPTH + 1) // 2
N_C_LAYERS = DEPTH // 2
RET_HEADS = 4
RET_DK = 128
RET_DV = 128
RET_WIDTH = RET_HEADS * RET_DV
RET_CHUNK = 128
ROPE_BASE = 10000.0
LRU_WIDTH = D_MODEL // 2
LRU_BLOCKS = 4
LRU_BLOCK = LRU_WIDTH // LRU_BLOCKS
LRU_C = 8.0
CONV_W = 4
MIX_WIDTH = RET_WIDTH + LRU_WIDTH
IN_SIZES = (RET_HEADS * RET_DK, RET_HEADS * RET_DK, RET_WIDTH, RET_WIDTH, LRU_WIDTH, LRU_WIDTH)
IN_COLS = sum(IN_SIZES)
IN_SPLITS = tuple(sum(IN_SIZES[:i + 1]) for i in range(len(IN_SIZES) - 1))
POOL_WINDOWS = (2, 4, 8, 16)
POOL_GROUP = D_MODEL // len(POOL_WINDOWS)
POOL_BUF = max(POOL_WINDOWS) - 1
D_FF = 2816
LN_EPS = 1e-5
DN_ALPHA = (2.0 * DEPTH) ** 0.25
DN_BETA = (8.0 * DEPTH) ** -0.25

kernel_name = 'hybrid_retention_rglru_pool_step'


def layer_norm(x, g, b):
    xf = x.astype(jnp.float32)
    mu = jnp.mean(xf, axis=-1, keepdims=True)
    var = jnp.mean(jnp.square(xf - mu), axis=-1, keepdims=True)
    return ((xf - mu) * lax.rsqrt(var + LN_EPS) * g + b).astype(x.dtype)


def swiglu_half(x, wg, wu, wd):
    return 0.5 * ((jax.nn.silu(x @ wg) * (x @ wu)) @ wd)


def rotary(x, pos):
    half = x.shape[-1] // 2
    inv = ROPE_BASE ** (-jnp.arange(half, dtype=jnp.float32) / half)
    ang = pos[:, None] * inv[None, :]
    cos = jnp.cos(ang)[None, :, None, :]
    sin = jnp.sin(ang)[None, :, None, :]
    x1, x2 = x[..., :half], x[..., half:]
    return jnp.concatenate([x1 * cos - x2 * sin, x2 * cos + x1 * sin], axis=-1)


def retention(q, k, v, s0):
    b_, t_ = q.shape[:2]
    c = RET_CHUNK if t_ % RET_CHUNK == 0 else t_
    nc = t_ // c
    lg = jnp.log1p(-jnp.exp2(-5.0 - jnp.arange(RET_HEADS, dtype=jnp.float32)))
    idx = jnp.arange(c, dtype=jnp.float32)
    diff = idx[:, None] - idx[None, :]
    decay = jnp.where(diff >= 0, jnp.exp(lg[:, None, None] * jnp.maximum(diff, 0.0)), 0.0)
    q_dec = jnp.exp(lg[:, None] * (idx + 1.0))[..., None]
    k_dec = jnp.exp(lg[:, None] * (c - 1.0 - idx))[..., None]
    c_dec = jnp.exp(lg * c)[:, None, None]

    def to_chunks(t):
        return t.reshape(b_, nc, c, RET_HEADS, t.shape[-1]).transpose(1, 0, 3, 2, 4)

    def step(s, qkv):
        qc, kc, vc = qkv
        scores = jnp.einsum('bhid,bhjd->bhij', qc, kc) * decay
        o = jnp.einsum('bhij,bhjv->bhiv', scores, vc) + jnp.einsum('bhid,bhdv->bhiv', qc * q_dec, s)
        s = s * c_dec + jnp.einsum('bhjd,bhjv->bhdv', kc * k_dec, vc)
        return s, o

    s_fin, o = lax.scan(step, s0, (to_chunks(q), to_chunks(k), to_chunks(v)))
    o = o.transpose(1, 0, 3, 2, 4).reshape(b_, t_, RET_HEADS, RET_DV)
    return o, s_fin


def causal_dwconv(u, buf, w, b):
    t_ = u.shape[1]
    ext = jnp.concatenate([buf.astype(u.dtype), u], axis=1)
    y = b + sum(ext[:, i:i + t_] * w[i] for i in range(CONV_W))
    return y, ext[:, -(CONV_W - 1):]


def rg_lru(u, h0, wa, ba, wi, bi, lam):
    b_, t_, _ = u.shape
    uf = u.astype(jnp.float32)
    ub = uf.reshape(b_, t_, LRU_BLOCKS, LRU_BLOCK)
    r = jax.nn.sigmoid(jnp.einsum('btnc,ncd->btnd', ub, wa).reshape(b_, t_, LRU_WIDTH) + ba)
    i = jax.nn.sigmoid(jnp.einsum('btnc,ncd->btnd', ub, wi).reshape(b_, t_, LRU_WIDTH) + bi)
    log_a = -LRU_C * jax.nn.softplus(-lam.astype(jnp.float32)) * r
    a = jnp.exp(log_a)
    xin = jnp.sqrt(-jnp.expm1(2.0 * log_a)) * (i * uf)

    def combine(lhs, rhs):
        a1, b1 = lhs
        a2, b2 = rhs
        return a1 * a2, a2 * b1 + b2

    a_cum, b_cum = lax.associative_scan(combine, (a, xin), axis=1)
    h = a_cum * h0.astype(jnp.float32)[:, None, :] + b_cum
    return h, h[:, -1]


def mix_ab(x, pos, s_ret, h0, conv_buf, w_in, w_out, gn_g, conv_w, conv_b, wa, ba, wi, bi, lam):
    b_, t_, _ = x.shape
    proj = x @ w_in
    q, k, v, g, ux, ug = jnp.split(proj, IN_SPLITS, axis=-1)
    q = rotary(q.reshape(b_, t_, RET_HEADS, RET_DK).astype(jnp.float32), pos) * (RET_DK ** -0.5)
    k = rotary(k.reshape(b_, t_, RET_HEADS, RET_DK).astype(jnp.float32), pos)
    v = v.reshape(b_, t_, RET_HEADS, RET_DV).astype(jnp.float32)
    o, s_new = retention(q, k, v, s_ret.astype(jnp.float32))
    mu = jnp.mean(o, axis=-1, keepdims=True)
    var = jnp.mean(jnp.square(o - mu), axis=-1, keepdims=True)
    o = ((o - mu) * lax.rsqrt(var + LN_EPS)).reshape(b_, t_, RET_WIDTH) * gn_g
    ret_out = (jax.nn.silu(g.astype(jnp.float32)) * o).astype(x.dtype)
    uc, conv_new = causal_dwconv(ux, conv_buf, conv_w, conv_b)
    h, h_new = rg_lru(uc, h0, wa, ba, wi, bi, lam)
    lru_out = (h * jax.nn.gelu(ug.astype(jnp.float32))).astype(x.dtype)
    y = jnp.concatenate([ret_out, lru_out], axis=-1) @ w_out
    return y, s_new.astype(x.dtype), h_new.astype(x.dtype), conv_new.astype(x.dtype)


def pool_mix(x, pos0, buf, w, bias, scale):
    b_, t_, d_ = x.shape
    ext = jnp.concatenate([buf.astype(x.dtype), x], axis=1)
    cs = jnp.cumsum(ext.astype(jnp.float32), axis=1)
    cs = jnp.concatenate([jnp.zeros((b_, 1, d_), jnp.float32), cs], axis=1)
    tpos = jnp.arange(t_, dtype=jnp.float32) + pos0
    xf = x.astype(jnp.float32)
    outs = []
    for gi, wnd in enumerate(POOL_WINDOWS):
        lo, hi = gi * POOL_GROUP, (gi + 1) * POOL_GROUP
        c = cs[:, :, lo:hi]
        s = c[:, POOL_BUF + 1:] - c[:, POOL_BUF + 1 - wnd:POOL_BUF + 1 - wnd + t_]
        cnt = jnp.minimum(float(wnd), tpos + 1.0)
        dlt = s / cnt[None, :, None] - xf[:, :, lo:hi]
        outs.append(jnp.einsum('btc,cd->btd', dlt, w[gi]))
    y = (jnp.concatenate(outs, axis=-1) + bias) * scale
    return y.astype(x.dtype), ext[:, -POOL_BUF:]


def trunk(x, pos0, ret0, h0, conv0, pool0, w_ffn_gate, w_ffn_up, w_ffn_down, ln_g, ln_b,
          w_mix_in, w_mix_out, ret_gn_g, lru_conv_w, lru_conv_b, lru_wa, lru_ba, lru_wi, lru_bi,
          lru_lambda, pool_w, pool_b, pool_scale):
    t_ = x.shape[1]
    pos = jnp.arange(t_, dtype=jnp.float32) + pos0
    rets, hs, convs, pools = [], [], [], []
    for layer in range(DEPTH):
        j = layer // 2
        x = layer_norm(DN_ALPHA * x + swiglu_half(x, w_ffn_gate[layer, 0], w_ffn_up[layer, 0], w_ffn_down[layer, 0]),
                       ln_g[layer, 0], ln_b[layer, 0])
        if layer % 2 == 0:
            y, s_new, h_new, c_new = mix_ab(x, pos, ret0[j], h0[j], conv0[j], w_mix_in[j], w_mix_out[j],
                                            ret_gn_g[j], lru_conv_w[j], lru_conv_b[j], lru_wa[j], lru_ba[j],
                                            lru_wi[j], lru_bi[j], lru_lambda[j])
            rets.append(s_new)
            hs.append(h_new)
            convs.append(c_new)
        else:
            y, p_new = pool_mix(x, pos0, pool0[j], pool_w[j], pool_b[j], pool_scale[j])
            pools.append(p_new)
        x = layer_norm(DN_ALPHA * x + y, ln_g[layer, 1], ln_b[layer, 1])
        x = layer_norm(DN_ALPHA * x + swiglu_half(x, w_ffn_gate[layer, 1], w_ffn_up[layer, 1], w_ffn_down[layer, 1]),
                       ln_g[layer, 2], ln_b[layer, 2])
    return x, jnp.stack(rets), jnp.stack(hs), jnp.stack(convs), jnp.stack(pools)


def setup_inputs(seed: int = 0) -> dict:
    key = jax.random.key(seed)
    ks = jax.random.split(key, 24)
    f32 = jnp.float32
    nrm = lambda k, s: jax.random.normal(k, s, f32)
    col_scale = jnp.concatenate([
        jnp.ones((2 * RET_HEADS * RET_DK,), f32), jnp.full((RET_WIDTH,), DN_BETA, f32),
        jnp.ones((RET_WIDTH,), f32), jnp.full((LRU_WIDTH,), DN_BETA, f32), jnp.ones((LRU_WIDTH,), f32)])
    a_target = jax.random.uniform(ks[20], (N_A_LAYERS, LRU_WIDTH), f32, 0.9, 0.999)
    sig = a_target ** (1.0 / LRU_C)
    return {
        'x_prompt': nrm(ks[0], (BATCH, SEQ, D_MODEL)),
        'x_sample': nrm(ks[1], (DEC_BATCH, DEC_SEQ, D_MODEL)),
        'state_ret': 0.5 * nrm(ks[2], (N_A_LAYERS, DEC_BATCH, RET_HEADS, RET_DK, RET_DV)),
        'state_lru_h': 0.5 * nrm(ks[3], (N_A_LAYERS, DEC_BATCH, LRU_WIDTH)),
        'state_lru_conv': nrm(ks[4], (N_A_LAYERS, DEC_BATCH, CONV_W - 1, LRU_WIDTH)),
        'state_pool': nrm(ks[5], (N_C_LAYERS, DEC_BATCH, POOL_BUF, D_MODEL)),
        'w_ffn_gate': nrm(ks[6], (DEPTH, 2, D_MODEL, D_FF)) * D_MODEL ** -0.5,
        'w_ffn_up': nrm(ks[7], (DEPTH, 2, D_MODEL, D_FF)) * (D_MODEL ** -0.5 * DN_BETA),
        'w_ffn_down': nrm(ks[8], (DEPTH, 2, D_FF, D_MODEL)) * (D_FF ** -0.5 * DN_BETA),
        'ln_g': 1.0 + 0.02 * nrm(ks[9], (DEPTH, 3, D_MODEL)),
        'ln_b': 0.02 * nrm(ks[10], (DEPTH, 3, D_MODEL)),
        'w_mix_in': nrm(ks[11], (N_A_LAYERS, D_MODEL, IN_COLS)) * D_MODEL ** -0.5 * col_scale,
        'w_mix_out': nrm(ks[12], (N_A_LAYERS, MIX_WIDTH, D_MODEL)) * (MIX_WIDTH ** -0.5 * DN_BETA),
        'ret_gn_g': 1.0 + 0.02 * nrm(ks[13], (N_A_LAYERS, RET_WIDTH)),
        'lru_conv_w': nrm(ks[14], (N_A_LAYERS, CONV_W, LRU_WIDTH)) * CONV_W ** -0.5,
        'lru_conv_b': 0.02 * nrm(ks[15], (N_A_LAYERS, LRU_WIDTH)),
        'lru_wa': nrm(ks[16], (N_A_LAYERS, LRU_BLOCKS, LRU_BLOCK, LRU_BLOCK)) * LRU_BLOCK ** -0.5,
        'lru_ba': 0.02 * nrm(ks[17], (N_A_LAYERS, LRU_WIDTH)),
        'lru_wi': nrm(ks[18], (N_A_LAYERS, LRU_BLOCKS, LRU_BLOCK, LRU_BLOCK)) * LRU_BLOCK ** -0.5,
        'lru_bi': 0.02 * nrm(ks[19], (N_A_LAYERS, LRU_WIDTH)),
        'lru_lambda': jnp.log(sig) - jnp.log1p(-sig),
        'pool_w': nrm(ks[21], (N_C_LAYERS, len(POOL_WINDOWS), POOL_GROUP, POOL_GROUP)) * (POOL_GROUP ** -0.5 * DN_BETA),
        'pool_b': 0.02 * nrm(ks[22], (N_C_LAYERS, D_MODEL)),
        'pool_scale': 1.0 + 0.1 * nrm(ks[23], (N_C_LAYERS, D_MODEL)),
    }


def reference(x_prompt, x_sample, state_ret, state_lru_h, state_lru_conv, state_pool,
              w_ffn_gate, w_ffn_up, w_ffn_down, ln_g, ln_b, w_mix_in, w_mix_out, ret_gn_g,
              lru_conv_w, lru_conv_b, lru_wa, lru_ba, lru_wi, lru_bi, lru_lambda,
              pool_w, pool_b, pool_scale):
    bp = x_prompt.shape[0]
    dt = x_prompt.dtype
    ret0_p = jnp.zeros((N_A_LAYERS, bp, RET_HEADS, RET_DK, RET_DV), dt)
    h0_p = jnp.zeros((N_A_LAYERS, bp, LRU_WIDTH), dt)
    conv0_p = jnp.zeros((N_A_LAYERS, bp, CONV_W - 1, LRU_WIDTH), dt)
    pool0_p = jnp.zeros((N_C_LAYERS, bp, POOL_BUF, D_MODEL), dt)
    y_prompt, ret_p, h_p, conv_p, pool_p = trunk(
        x_prompt, 0, ret0_p, h0_p, conv0_p, pool0_p, w_ffn_gate, w_ffn_up, w_ffn_down, ln_g, ln_b,
        w_mix_in, w_mix_out, ret_gn_g, lru_conv_w, lru_conv_b, lru_wa, lru_ba, lru_wi, lru_bi,
        lru_lambda, pool_w, pool_b, pool_scale)
    y_sample, ret_s, h_s, conv_s, pool_s = trunk(
        x_sample, PAST_LEN, state_ret, state_lru_h, state_lru_conv, state_pool, w_ffn_gate, w_ffn_up,
        w_ffn_down, ln_g, ln_b, w_mix_in, w_mix_out, ret_gn_g, lru_conv_w, lru_conv_b, lru_wa, lru_ba,
        lru_wi, lru_bi, lru_lambda, pool_w, pool_b, pool_scale)
    return (y_prompt, y_sample, ret_p, h_p, conv_p, pool_p, ret_s, h_s, conv_s, pool_s)
```

```python
import contextlib
import math
import os
SKIP = set(os.environ.get('KSKIP', '').split(','))
import numpy as np
import concourse.bass as bass
import concourse.mybir as mybir
from concourse.bass_utils import run_bass_kernel_spmd

F32 = mybir.dt.float32
BF16 = mybir.dt.bfloat16
AF = mybir.ActivationFunctionType
ALU = mybir.AluOpType

ENGS = ("pe", "act", "dve", "pool", "sp")


class Op:
    __slots__ = ("eng", "idx", "emit", "deps", "inc", "dma", "dma_val", "ndma", "tag")


class Sched:
    def __init__(self, nc):
        self.nc = nc
        self.streams = {e: [] for e in ENGS}
        self.lastw = {}
        self.readers = {}
        self.dma_cnt = {}
        self.all_ops = []

    def add(self, eng, emit, reads=(), writes=(), dma=None, ndma=1, tag=None):
        op = Op()
        op.eng = eng
        op.emit = emit
        op.inc = False
        op.dma = dma
        op.ndma = ndma
        op.tag = tag
        writes = list(writes) + [k for k in reads if isinstance(k, tuple) and k[0] == "ps" and k not in writes]
        reads = [k for k in reads if not (isinstance(k, tuple) and k[0] == "ps")]
        for k in reads + writes:
            if isinstance(k, tuple) and k[0] == "A":
                reads.append("EPOCH")
                break
        deps = set()
        for k in reads:
            w = self.lastw.get(k)
            if w is not None:
                deps.add(w)
        for k in writes:
            w = self.lastw.get(k)
            if w is not None:
                deps.add(w)
            rd = self.readers.get(k)
            if rd:
                deps.update(rd.values())
        rk = ("d", len(self.all_ops)) if dma is not None else eng
        for k in reads:
            self.readers.setdefault(k, {})[rk] = op
        for k in writes:
            self.lastw[k] = op
            self.readers[k] = {}
        if dma is not None:
            c = self.dma_cnt.get(dma, 0) + ndma
            self.dma_cnt[dma] = c
            op.dma_val = 16 * c
        elif eng == "pe":
            deps = {d for d in deps if not (d.eng == "pe" and d.dma is None)}
        deps.discard(op)
        op.deps = deps
        st = self.streams[eng]
        st.append(op)
        op.idx = len(st)
        self.all_ops.append(op)
        return op

    def pe(self, emit, reads=(), writes=(), **kw):
        return self.add("pe", emit, reads, writes, **kw)

    def act(self, emit, reads=(), writes=(), **kw):
        return self.add("act", emit, reads, writes, **kw)

    def dve(self, emit, reads=(), writes=(), **kw):
        return self.add("dve", emit, reads, writes, **kw)

    def finalize_and_emit(self, final_wait_eng="sp"):
        nc = self.nc
        for op in self.all_ops:
            for d in op.deps:
                if d.dma is None:
                    d.inc = True
        with contextlib.ExitStack() as es:
            esem = {e: es.enter_context(nc.semaphore("s_" + e)) for e in ENGS if e != "sp"}
            dsem = {k: es.enter_context(nc.semaphore("d_" + str(k))) for k in self.dma_cnt}
            last_vals = {}
            val = {}
            for e in ENGS:
                if e == "sp":
                    continue
                lst = [op for op in self.streams[e] if op.dma is None]
                if lst:
                    lst[-1].inc = True
                c = 0
                for op in self.streams[e]:
                    if op.dma is None and op.inc:
                        c += 1
                    val[op] = c
                if lst:
                    last_vals[e] = val[lst[-1]]
            engobj = {"pe": "tensor", "act": "scalar", "dve": "vector", "pool": "gpsimd", "sp": "sync"}
            blk = es.enter_context(nc.Block())

            def make(e):
                def body(eng):
                    waited = {}
                    for op in self.streams[e]:
                        need = {}
                        for d in op.deps:
                            if d.dma is not None:
                                s = dsem[d.dma]
                                v = d.dma_val
                            else:
                                s = esem[d.eng]
                                v = val[d]
                            key = id(s)
                            if key not in need or need[key][1] < v:
                                need[key] = (s, v)
                        for key, (s, v) in need.items():
                            if waited.get(key, 0) >= v:
                                continue
                            eng.wait_ge(s, v)
                            waited[key] = v
                        r = op.emit(eng)
                        if op.dma is not None:
                            if not isinstance(r, (list, tuple)):
                                r = [r]
                            assert len(r) == op.ndma, (len(r), op.ndma, op.tag)
                            for ins in r:
                                ins.then_inc(dsem[op.dma], 16)
                        elif op.inc:
                            r.then_inc(esem[e], 1)
                    if e == final_wait_eng:
                        for k, c in self.dma_cnt.items():
                            eng.wait_ge(dsem[k], 16 * c)
                        for e2, v in last_vals.items():
                            eng.wait_ge(esem[e2], v)

                return body

            for e in ENGS:
                getattr(blk, engobj[e])(make(e))


D = 1024
KC = 8
DFF = 2816
NFC = 22
ALPHA = (2.0 * 4) ** 0.25
LN_EPS = 1e-5
PAD = 16
NS = 16
PAST = 16384
HALVES = ((0, 12), (12, 22))
GELU_C = math.sqrt(2.0 / math.pi)

P_LNG = 0
P_LNB = 96
P_GNG = 192
P_CW = 200
P_CB = 232
P_BA = 240
P_BI = 248
P_LAM = 256
P_PB = 264
P_PS = 280
P_ROWS = 384


class Cfg:
    def __init__(self, SEQ=2048, DEPTH=4, mixers=True, mixA=None, mixC=None):
        self.SEQ = SEQ
        self.DEPTH = DEPTH
        self.mixers = mixers
        self.mixA = mixers if mixA is None else mixA
        self.mixC = mixers if mixC is None else mixC


def build_program(cfg):
    SEQ = cfg.SEQ
    DEPTH = cfg.DEPTH
    NT = SEQ // 512
    TTP = PAD + SEQ + NS
    CS = PAD + SEQ
    tiles = [(PAD + 512 * i, 512) for i in range(NT)] + [(CS, NS)]
    NTL = len(tiles)
    NA = (DEPTH + 1) // 2
    NCL = DEPTH // 2

    nc = bass.Bass("TRN2", target_bir_lowering=False)

    class _Lazy:
        def __init__(self, name, shape, kind="ExternalInput"):
            self.name, self.shape, self.ap_, self.kind = name, shape, None, kind

        def get(self):
            if self.ap_ is None:
                self.ap_ = nc.dram_tensor(self.name, list(self.shape), F32, kind=self.kind).ap()
            return self.ap_

        def __getitem__(self, k):
            return self.get()[k]

        def rearrange(self, *a, **kw):
            return self.get().rearrange(*a, **kw)

    def din(name, shape, dtype=F32):
        return _Lazy(name, shape)

    def dout(name, shape, dtype=F32):
        return _Lazy(name, shape, "ExternalOutput")

    xp = din("xp", [SEQ, D])
    xs = din("xs", [NS, D])
    st_ret = din("st_ret", [2, NS, 4, 128, 128])
    st_h = din("st_h", [2, NS, 512])
    st_conv = din("st_conv", [2, NS, 3, 512])
    st_pool = din("st_pool", [2, NS, 15, D])
    NWL = max(DEPTH, 1) if getattr(cfg, "tinyw", False) else 4
    wg = din("wg", [NWL, 2, D, DFF])
    wu = din("wu", [NWL, 2, D, DFF])
    wd = din("wd", [NWL, 2, DFF, D])
    w_in = din("w_in", [2, D, 3072])
    w_out = din("w_out", [2, D, D])
    w_insw = din("w_insw", [2, D, 1024])
    lru_wa = din("lru_wa", [2, 4, 128, 128])
    lru_wi = din("lru_wi", [2, 4, 128, 128])
    pool_w = din("pool_w", [2, 4, 256, 256])
    params = din("params", [P_ROWS, 128])
    ident_d = din("ident", [128, 128])
    costab = din("costab", [128, SEQ + NS])
    sintab = din("sintab", [128, SEQ + NS])
    decay_d = din("decayT", [128, 512])
    qdec_d = din("qdec", [128, 512])
    kdec_d = din("kdec", [128, 512])
    cdec_d = din("cdec", [128, 512])
    gam_d = din("gamtab", [128, 512])
    kdecs_d = din("kdecs", [128, 4])
    selp_d = din("selp", [120, 2, 4, NS])
    icnt_d = din("icnt", [128, 4, 16])

    y_p = dout("y_p", [SEQ, D])
    y_s = dout("y_s", [NS, D])
    ret_p = dout("ret_p", [2, 4, 128, 128])
    h_p = dout("h_p", [2, 512])
    conv_p = dout("conv_p", [2, 3, 512])
    pool_p = dout("pool_p", [2, 15, D])
    ret_s = dout("ret_s", [2, NS, 4, 128, 128])
    h_s = dout("h_s", [2, NS, 512])
    conv_s = dout("conv_s", [2, NS, 3, 512])
    pool_s = dout("pool_s", [2, NS, 15, D])

    with contextlib.ExitStack() as es:
        def sb(name, shape, dtype):
            return es.enter_context(nc.sbuf_tensor("sb_" + name, list(shape), dtype))

        xT = sb("xT", [128, KC, TTP], F32)
        xb = sb("xb", [128, KC, TTP], BF16)
        ARENA_COLS = max(12 * TTP, 24960)
        arena = sb("arena", [128, ARENA_COLS], BF16)
        ring = sb("ring", [128, 4, 4096], BF16)
        stage = sb("stage", [128, 2, D], F32)
        sgb = sb("sgb", [128, 2, 512], F32)
        zsq = sb("zsq", [128, 2, 512], BF16)
        lnA = sb("lnA", [128, 512], F32)
        lnB = sb("lnB", [128, 512], F32)
        lnC = sb("lnC", [128, 512], F32)
        lnT = sb("lnT", [128, 2, 512], F32)
        PRM = sb("PRM", [128, P_ROWS], F32)
        DRV = sb("DRV", [128, 64], F32)
        ident = sb("ident", [128, 128], F32)
        identb = sb("identb", [128, 128], BF16)
        onesb = sb("onesb", [128, 128], BF16)
        cst = sb("cst", [128, 8], F32)
        ps = [es.enter_context(nc.psum_tensor("ps%d" % i, [128, 512], F32)) for i in range(8)]

        S = Sched(nc)

        def MM(out, lhsT, rhs, start, stop, r, w):
            S.pe(lambda e: e.matmul(out, lhsT, rhs, start=start, stop=stop), r, w)

        def TR(out, in_, idn, r, w):
            S.pe(lambda e: e.transpose(out, in_, idn), r, w)

        def ACT(out, in_, func, r, w, bias=None, scale=None):
            kw = {}
            if bias is not None:
                kw["bias"] = bias
            if scale is not None:
                kw["scale"] = scale
            S.act(lambda e: e.activation(out, in_, func, **kw), r, w)

        def ACOPY(out, in_, r, w):
            S.act(lambda e: e.copy(out, in_), r, w)

        def VCOPY(out, in_, r, w):
            S.dve(lambda e: e.tensor_copy(out, in_), r, w)

        def TT(out, a, b, op, r, w):
            S.dve(lambda e: e.tensor_tensor(out, a, b, op), r, w)

        def PTT(out, a, b, op, r, w):
            S.add("pool", lambda e: e.tensor_tensor(out, a, b, op), r, w)

        def PCOPY(out, in_, r, w):
            S.add("pool", lambda e: e.tensor_copy(out, in_), r, w)

        def TS(out, a, s1, s2, op0, op1, r, w):
            if op1 is None:
                S.dve(lambda e: e.tensor_scalar(out, a, s1, None, op0), r, w)
            else:
                S.dve(lambda e: e.tensor_scalar(out, a, s1, s2, op0, op1), r, w)

        def STT(out, a, sc, b, op0, op1, r, w):
            S.dve(lambda e: e.scalar_tensor_tensor(out, a, sc, b, op0, op1), r, w)

        def MEMSET(out, v, w, eng="dve"):
            S.add(eng, lambda e: e.memset(out, v), (), w)

        def DMA(eng, out, in_, r, w, key):
            S.add(eng, lambda e: e.dma_start(out=out, in_=in_), r, w, dma=key)

        def DMAS(eng, pairs, r, w, key):
            pairs = list(pairs)
            S.add(eng, lambda e: [e.dma_start(out=o, in_=i) for (o, i) in pairs], r, w, dma=key, ndma=len(pairs))

        ring_n = [0]

        def ring_fill(pairs_fn):
            slot = ring_n[0] % 4
            ring_n[0] += 1
            pairs = pairs_fn(ring[:, slot, :])
            DMAS("pool", pairs, (), [("w", slot)], "w%d" % slot)
            return slot

        DMAS("sp", [(ident[:], ident_d.get())], (), ["identraw"], "c0")
        if 'ms1' not in SKIP:
            MEMSET(cst[:, 0:1], LN_EPS, ["cst"])
            MEMSET(cst[:, 1:2], 1.0, ["cst"])
            MEMSET(cst[:, 2:3], 1e-20, ["cst"])
            MEMSET(cst[:, 3:4], 1.0 + 1e-12, ["cst"])
        if 'ms2' not in SKIP:
            MEMSET(onesb[:], 1.0 / D, ["onesb"])
        if 'ms3' not in SKIP:
            MEMSET(xT[:, :, 0:PAD], 0.0, [("x", k, 0) for k in range(KC)])
        if 'ms4' not in SKIP:
            MEMSET(xb[:, :, 0:PAD], 0.0, [("xb", k, 0) for k in range(KC)])
        if 'idb' not in SKIP:
            ACOPY(identb[:], ident[:], ["identraw"], ["identb"])
        for g in range(3 if 'prm' not in SKIP else 0):
            DMA("sp", stage[:, 0, g * 128:(g + 1) * 128], params[g * 128:(g + 1) * 128, :], (), [("stage", 0)], "stg0")
        for g in range(3 if 'prm' not in SKIP else 0):
            TR(ps[0][:, g * 128:(g + 1) * 128], stage[:, 0, g * 128:(g + 1) * 128], ident[:], [("stage", 0), "identraw"], [("ps", 0)])
        if 'prm' not in SKIP:
            ACOPY(PRM[:, 0:384], ps[0][:, 0:384], [("ps", 0)], ["PRM"])
        D_NBA, D_NBI, D_C1, D_PBS = 0, 8, 16, 24
        if 'drv' not in SKIP:
            TS(DRV[:, D_NBA:D_NBA + 8], PRM[:, P_BA:P_BA + 8], -1.0, None, ALU.mult, None, ["PRM"], ["DRV"])
            TS(DRV[:, D_NBI:D_NBI + 8], PRM[:, P_BI:P_BI + 8], -1.0, None, ALU.mult, None, ["PRM"], ["DRV"])
            ACT(DRV[:, 40:48], PRM[:, P_LAM:P_LAM + 8], AF.Exp, ["PRM", "DRV"], ["DRV"], scale=-1.0)
            ACT(DRV[:, 40:48], DRV[:, 40:48], AF.Ln, ["DRV", "cst"], ["DRV"], bias=cst[:, 1:2])
            TS(DRV[:, D_C1:D_C1 + 8], DRV[:, 40:48], -8.0, None, ALU.mult, None, ["DRV"], ["DRV"])
            TT(DRV[:, D_PBS:D_PBS + 16], PRM[:, P_PB:P_PB + 16], PRM[:, P_PS:P_PS + 16], ALU.mult, ["PRM", "DRV"], ["DRV"])

        nblk = SEQ // 128
        for b in range(nblk):
            sk = b % 2
            t = b // 4
            c = PAD + b * 128
            DMA("sp", stage[:, sk, :], xp[b * 128:(b + 1) * 128, :], (), [("stage", sk)], "stg%d" % sk)
            for half in range(2):
                bank = 2 * sk + half
                for q in range(4):
                    kc = half * 4 + q
                    TR(ps[bank][:, q * 128:(q + 1) * 128], stage[:, sk, kc * 128:(kc + 1) * 128], ident[:],
                       [("stage", sk), "identraw"], [("ps", bank)])
                src = ps[bank][:, :].rearrange("p (k c) -> p k c", k=4)
                ACOPY(xT[:, half * 4:half * 4 + 4, c:c + 128], src, [("ps", bank)], [("x", half * 4 + q, t) for q in range(4)])
                VCOPY(xb[:, half * 4:half * 4 + 4, c:c + 128], src, [("ps", bank)], [("xb", half * 4 + q, t) for q in range(4)])
        if 'sin' not in SKIP:
            DMA("sp", stage[0:NS, 0, :], xs.get(), (), [("stage", 0)], "stg0")
        for kc in range(KC if 'sin' not in SKIP else 0):
            TR(ps[0][:, kc * NS:(kc + 1) * NS], stage[0:NS, 0, kc * 128:(kc + 1) * 128], ident[0:NS, 0:NS],
               [("stage", 0), "identraw"], [("ps", 0)])
        srcs = ps[0][:, 0:KC * NS].rearrange("p (k c) -> p k c", k=KC)
        if 'sin' not in SKIP:
            ACOPY(xT[:, :, CS:CS + NS], srcs, [("ps", 0)], [("x", k, NT) for k in range(KC)])
            VCOPY(xb[:, :, CS:CS + NS], srcs, [("ps", 0)], [("xb", k, NT) for k in range(KC)])

        def resid_evac(t, m, ybank, mode):
            c0, W = tiles[t]
            xs_ = xT[:, m, c0:c0 + W]
            if mode == "first":
                STT(xs_, xs_, ALPHA, ps[ybank][:, :W], ALU.mult, ALU.add, [("ps", ybank), ("x", m, t)], [("x", m, t)])
            else:
                TT(xs_, xs_, ps[ybank][:, :W], ALU.add, [("ps", ybank), ("x", m, t)], [("x", m, t)])

        ln_par = [0]

        def ln_banks():
            return (6, 7) if ln_par[0] % 2 == 0 else (2, 3)

        def ln_prep(t, m, on_pool=False):
            c0, W = tiles[t]
            k = m % 2
            if on_pool and m % 2 == 1:
                ACOPY(xb[:, m, c0:c0 + W], xT[:, m, c0:c0 + W], [("x", m, t)], [("xb", m, t)])
            else:
                VCOPY(xb[:, m, c0:c0 + W], xT[:, m, c0:c0 + W], [("x", m, t)], [("xb", m, t)])
            ACT(zsq[:, k, :W], xT[:, m, c0:c0 + W], AF.Square, [("x", m, t)], [("zsq", k)])
            ln_drip(1)

        def ln_stats(t, m):
            c0, W = tiles[t]
            k = m % 2
            b6, b7 = ln_banks()
            MM(ps[b6][:, :W], onesb[:], xb[:, m, c0:c0 + W], m == 0, m == KC - 1, ["onesb", ("xb", m, t)], [("ps", b6)])
            MM(ps[b7][:, :W], onesb[:], zsq[:, k, :W], m == 0, m == KC - 1, ["onesb", ("zsq", k)], [("ps", b7)])

        ln_pending = []

        def ln_drip(n):
            for _ in range(n):
                if ln_pending:
                    ln_pending.pop(0)()

        def ln_flush():
            while ln_pending:
                ln_pending.pop(0)()

        def ln_finalize(t, l, idx, defer=False):
            c0, W = tiles[t]
            row = (l * 3 + idx) * 8
            b6, b7 = ln_banks()
            ln_par[0] += 1
            ln_flush()
            ACT(lnA[:, :W], ps[b6][:, :W], AF.Square, [("ps", b6)], ["lnA"])
            TT(lnA[:, :W], ps[b7][:, :W], lnA[:, :W], ALU.subtract, [("ps", b7), "lnA"], ["lnA"])
            ACT(lnB[:, :W], lnA[:, :W], AF.Ln, ["lnA", "cst"], ["lnB"], bias=cst[:, 0:1])
            ACT(lnB[:, :W], lnB[:, :W], AF.Exp, ["lnB"], ["lnB"], scale=-0.5)
            TT(lnC[:, :W], ps[b6][:, :W], lnB[:, :W], ALU.mult, [("ps", b6), "lnB"], ["lnC"])
            def piece(m):
                k = m % 2
                xs_ = xT[:, m, c0:c0 + W]
                TT(lnT[:, k, :W], xs_, lnB[:, :W], ALU.mult, [("x", m, t), "lnB"], [("lnT", k)])
                TT(lnT[:, k, :W], lnT[:, k, :W], lnC[:, :W], ALU.subtract, [("lnT", k), "lnC"], [("lnT", k)])
                g_ap = PRM[:, P_LNG + row + m:P_LNG + row + m + 1]
                b_ap = PRM[:, P_LNB + row + m:P_LNB + row + m + 1]
                ACT(xs_, lnT[:, k, :W], AF.Identity, [("lnT", k), "PRM"], [("x", m, t)], bias=b_ap, scale=g_ap)
                ACT(xb[:, m, c0:c0 + W], lnT[:, k, :W], AF.Identity, [("lnT", k), "PRM"], [("xb", m, t)], bias=b_ap, scale=g_ap)

            for m in range(KC):
                ln_pending.append(lambda m=m: piece(m))
            if not defer:
                ln_flush()

        aT = arena[:, 0:12 * TTP].rearrange("p (f c) -> p f c", f=12)
        cnt = {"gu": 0, "y": 0}

        def ffn(l, i, ln_idx):
            for hf, (f0, f1) in enumerate(HALVES):
                nf = f1 - f0
                for blk0 in range(f0, f1, 2):
                    nb = min(2, f1 - blk0)

                    def pairs(slotv, blk0=blk0, nb=nb):
                        gv = slotv[:, 0:2048].rearrange("p (k c) -> p k c", k=KC)[:, :, 0:nb * 128]
                        uv = slotv[:, 2048:4096].rearrange("p (k c) -> p k c", k=KC)[:, :, 0:nb * 128]
                        gs = wg[l, i].rearrange("(k p) c -> p k c", p=128)[:, :, blk0 * 128:(blk0 + nb) * 128]
                        us = wu[l, i].rearrange("(k p) c -> p k c", p=128)[:, :, blk0 * 128:(blk0 + nb) * 128]
                        return [(gv, gs), (uv, us)]

                    slot = ring_fill(pairs)
                    gview = ring[:, slot, 0:2048].rearrange("p (k c) -> p k c", k=KC)
                    uview = ring[:, slot, 2048:4096].rearrange("p (k c) -> p k c", k=KC)
                    for t, (c0, W) in enumerate(tiles):
                        for j in range(nb):
                            fl = blk0 + j - f0
                            kk = cnt["gu"] % 2
                            cnt["gu"] += 1
                            gb, ub = kk, 2 + kk
                            for kc in range(KC):
                                MM(ps[gb][:, :W], gview[:, kc, j * 128:(j + 1) * 128], xb[:, kc, c0:c0 + W], kc == 0, kc == KC - 1,
                                   [("w", slot), ("xb", kc, t)], [("ps", gb)])
                            for kc in range(KC):
                                MM(ps[ub][:, :W], uview[:, kc, j * 128:(j + 1) * 128], xb[:, kc, c0:c0 + W], kc == 0, kc == KC - 1,
                                   [("w", slot), ("xb", kc, t)], [("ps", ub)])
                            ACT(sgb[:, kk, :W], ps[gb][:, :W], AF.Silu, [("ps", gb)], [("sg", kk)])
                            STT(aT[:, fl, c0:c0 + W], sgb[:, kk, :W], 0.5, ps[ub][:, :W], ALU.mult, ALU.mult, [("sg", kk), ("ps", ub)], [("A", "aT", fl, t)])
                slots = []
                for g0 in range(0, nf, 4):
                    ng = min(4, nf - g0)

                    def pairs(slotv, g0=g0, ng=ng):
                        dv = slotv[:, 0:ng * 1024].rearrange("p (f c) -> p f c", f=ng)
                        src = wd[l, i, (f0 + g0) * 128:(f0 + g0 + ng) * 128, :].rearrange("(f p) c -> p f c", p=128)
                        return [(dv, src)]

                    slots.append(ring_fill(pairs))
                for t, (c0, W) in enumerate(tiles):
                    for m in range(KC):
                        yb = (4, 5, 0, 1)[cnt["y"] % 4]
                        cnt["y"] += 1
                        for fi in range(nf):
                            slot = slots[fi // 4]
                            wv = ring[:, slot, (fi % 4) * 1024 + m * 128:(fi % 4) * 1024 + (m + 1) * 128]
                            MM(ps[yb][:, :W], wv, aT[:, fi, c0:c0 + W], fi == 0, fi == nf - 1,
                               [("w", slot), ("A", "aT", fi, t)], [("ps", yb)])
                        resid_evac(t, m, yb, "first" if hf == 0 else "acc")
                        if hf == 1:
                            ln_prep(t, m, on_pool=True)
                            if m > 0:
                                ln_stats(t, m - 1)
                    if hf == 1:
                        ln_stats(t, KC - 1)
                        ln_finalize(t, l, ln_idx, defer=True)
                if hf == 1:
                    ln_flush()


        misc = sb("misc", [128, 32], F32)
        carve_off = [0]

        def carve_reset(off=0):
            carve_off[0] = off

        def carve(ncols, dtype=BF16):
            n16 = ncols * (2 if dtype == F32 else 1)
            n16 = (n16 + 15) // 16 * 16
            o = carve_off[0]
            assert o + n16 <= ARENA_COLS, ("arena overflow", o, n16, ARENA_COLS)
            carve_off[0] = o + n16
            v = arena[:, o:o + n16]
            if dtype == F32:
                v = v.bitcast(F32)
            return v[:, 0:ncols]

        def barrier():
            S.add("dve", lambda e: e.memset(misc[:, 31:32], 0.0), (), ["EPOCH"])

        def pool_mixer(l):
            j = l // 2
            barrier()
            carve_reset()
            dlt = [carve(8 * 512).rearrange("p (k c) -> p k c", k=8) for _ in range(2)]
            pq = [carve(2 * 528, F32).rearrange("p (a c) -> p a c", a=2) for _ in range(2)]
            ostg = carve(1024, F32)
            dls = carve(8 * NS).rearrange("p (k c) -> p k c", k=8)
            sums = carve(8 * NS, F32).rearrange("p (k c) -> p k c", k=8)
            selp = carve(2 * 4 * NS, F32).rearrange("p (h g c) -> p h g c", h=2, g=4)
            icnt = carve(64, F32).rearrange("p (g c) -> p g c", g=4)
            t16 = carve(16, F32)

            def pairs(slotv):
                dv = slotv[:, 0:2048].rearrange("p (g c) -> p g c", g=8)
                src = pool_w[j].rearrange("g (k p) n -> p (g k) n", p=128)
                return [(dv, src)]

            slot = ring_fill(pairs)
            pwv = ring[:, slot, 0:2048].rearrange("p (g c) -> p g c", g=8)
            DMAS("sp", [(selp[0:120], selp_d.get()), (icnt, icnt_d.get())], (), [("A", "selp")], "pc%d" % j)
            prow = P_PS + j * 8
            WIN = (2, 4, 8, 16)

            def mm_resid_ln(t, dl):
                c0, W = tiles[t]
                for m in range(KC):
                    g, mm_ = m // 2, m % 2
                    yb = 4 + cnt["y"] % 2
                    cnt["y"] += 1
                    for k in range(2):
                        MM(ps[yb][:, :W], pwv[:, g * 2 + k, mm_ * 128:(mm_ + 1) * 128], dl[:, 2 * g + k, :W], k == 0, k == 1,
                           [("w", slot), ("A", "dlt", id(dl))], [("ps", yb)])
                    xs_ = xT[:, m, c0:c0 + W]
                    ACT(xs_, xs_, AF.Identity, [("x", m, t), "DRV"], [("x", m, t)], bias=DRV[:, D_PBS + j * 8 + m:D_PBS + j * 8 + m + 1], scale=ALPHA)
                    STT(xs_, ps[yb][:, :W], PRM[:, prow + m:prow + m + 1], xs_, ALU.mult, ALU.add, [("ps", yb), ("x", m, t), "PRM"], [("x", m, t)])
                    ln_prep(t, m)
                    if m > 0:
                        ln_stats(t, m - 1)
                ln_stats(t, KC - 1)
                ln_finalize(t, l, 1, defer=True)

            def tok_rows_out(csrc, n, dst_ap_fn, key):
                for half in range(2):
                    for q in range(4):
                        kc = half * 4 + q
                        TR(ps[half][0:16, q * 128:(q + 1) * 128], xT[:, kc, csrc:csrc + 16], ident[:], [("x", kc, n), "identraw"], [("ps", half)])
                    ACOPY(ostg[0:16, half * 512:(half + 1) * 512], ps[half][0:16, :], [("ps", half)], [("A", "ostg")])
                o_ap, i_ap = dst_ap_fn(ostg)
                DMA("sp", o_ap, i_ap, [("A", "ostg")], (), key)

            DMAS("sp", [(stage[0:120, h, :], st_pool[j].rearrange("b i c -> (b i) c")[h * 120:(h + 1) * 120, :]) for h in range(2)],
                 (), [("stage", 0), ("stage", 1)], "stgP")
            for kc in range(KC):
                g = kc // 2
                for h in range(2):
                    MM(ps[2][:, kc * NS:(kc + 1) * NS], stage[0:120, h, kc * 128:(kc + 1) * 128], selp[0:120, h, g, :], h == 0, h == 1,
                       [("stage", 0), ("stage", 1), ("A", "selp")], [("ps", 2)])
            xs3 = xT[:, :, CS:CS + NS]
            TT(sums, xs3, ps[2][:, 0:KC * NS].rearrange("p (k c) -> p k c", k=KC), ALU.add, [("ps", 2)] + [("x", k, NT) for k in range(KC)], [("A", "sums")])
            for g in range(4):
                STT(dls[:, 2 * g:2 * g + 2, :], sums[:, 2 * g:2 * g + 2, :], 1.0 / WIN[g], xT[:, 2 * g:2 * g + 2, CS:CS + NS], ALU.mult, ALU.subtract,
                    [("A", "sums"), ("x", 2 * g, NT), ("x", 2 * g + 1, NT)], [("A", "dlt", id(dls))])
            DMA("sp", pool_s[j][:, 0:14, :], st_pool[j][:, 1:15, :], (), (), "pcp%d" % j)
            tok_rows_out(CS, NT, lambda o: (pool_s[j][:, 14, :], o[0:16, :]), "ostg")
            mm_resid_ln(NT, dls)

            def windows(t):
                c0, W = tiles[t]
                dl = dlt[t % 2]
                for g in range(4):
                    w = WIN[g]
                    k0 = 2 * g
                    rd = [("x", kc, tt) for kc in (k0, k0 + 1) for tt in ((t, t - 1) if t > 0 else (t,))]
                    L = W + w - 2
                    TT(pq[0][:, :, 0:L], xT[:, k0:k0 + 2, c0 - (w - 2):c0 + W], xT[:, k0:k0 + 2, c0 - (w - 1):c0 + W - 1], ALU.add, rd, [("A", "pq", 0)])
                    cur, sh = 0, 2
                    while sh < w:
                        L2 = L - sh
                        TT(pq[1 - cur][:, :, 0:L2], pq[cur][:, :, sh:sh + L2], pq[cur][:, :, 0:L2], ALU.add, [("A", "pq", cur)], [("A", "pq", 1 - cur)])
                        cur, L, sh = 1 - cur, L2, sh * 2
                    STT(dl[:, k0:k0 + 2, :W], pq[cur][:, :, 0:W], 1.0 / w, xT[:, k0:k0 + 2, c0:c0 + W], ALU.mult, ALU.subtract,
                        [("A", "pq", cur), ("x", k0, t), ("x", k0 + 1, t)], [("A", "dlt", id(dl))])
                    if t == 0:
                        for a in range(2):
                            kc = k0 + a
                            TT(t16[:, :], pq[cur][:, a, 0:16], icnt[:, g, :], ALU.mult, [("A", "pq", cur), ("A", "selp")], [("A", "t16")])
                            TT(dl[:, kc, 0:16], t16[:, :], xT[:, kc, c0:c0 + 16], ALU.subtract, [("A", "t16"), ("x", kc, t)], [("A", "dlt", id(dl))])

            windows(0)
            for t in range(NT):
                if t + 1 < NT:
                    windows(t + 1)
                else:
                    cl = PAD + SEQ - 16
                    tok_rows_out(cl, NT - 1, lambda o: (pool_p[j], o[1:16, :]), "ostg")
                mm_resid_ln(t, dlt[t % 2])
            ln_flush()

        LG = [float(np.log1p(-np.exp2(np.float32(-5.0 - h))).astype(np.float32)) for h in range(4)]
        CDEC = [float(np.exp(np.float32(LG[h]) * np.float32(128.0))) for h in range(4)]
        GAM = [float(np.exp(np.float32(LG[h]))) for h in range(4)]
        QSC = float(128 ** -0.5)

        def slot_view(slot):
            return ring[:, slot, :].rearrange("p (k c) -> p k c", k=KC)

        def fill_cols(j, base):
            def pairs(slotv):
                return [(slotv.rearrange("p (k c) -> p k c", k=KC), w_in[j].rearrange("(k p) c -> p k c", p=128)[:, :, base:base + 512])]
            return ring_fill(pairs)

        def fill_cols_swapped(j, base):
            def pairs(slotv):
                return [(slotv.rearrange("p (k c) -> p k c", k=KC), w_insw[j].rearrange("(k p) c -> p k c", p=128)[:, :, base:base + 512])]
            return ring_fill(pairs)

        def proj(slot, col0, bank, t, ncols=128):
            c0, W = tiles[t]
            sv = slot_view(slot)
            for kc in range(KC):
                MM(ps[bank][0:ncols, :W], sv[:, kc, col0:col0 + ncols], xb[:, kc, c0:c0 + W], kc == 0, kc == KC - 1,
                   [("w", slot), ("xb", kc, t)], [("ps", bank)])

        def mixer_a(l):
            j = l // 2
            barrier()
            carve_reset()
            Sst = carve(512, F32)
            Sbf = carve(512)
            decayT = carve(512, F32)
            qdec = carve(512, F32)
            wab = carve(512)
            wib = carve(512)
            o128 = carve(128)
            sgT = carve(4 * 512).rearrange("p (h c) -> p h c", h=4)
            mixT = carve(8 * 512).rearrange("p (k c) -> p k c", k=8)
            base = carve_off[0]
            hprev = misc[:, 0:4]
            uxprev = misc[:, 4:16].rearrange("p (n i) -> p n i", n=4)
            kdecs = misc[:, 16:20]
            AK = lambda *k: ("A",) + k

            DMAS("sp", [(decayT, decay_d.get()), (qdec, qdec_d.get()), (kdecs, kdecs_d.get())], (), [AK("dtab"), "kdecs"], "ma%d" % j)
            MEMSET(Sst, 0.0, [AK("Sst")])
            MEMSET(Sbf, 0.0, [AK("Sbf")])
            MEMSET(o128, 1.0 / 128, [AK("o128")])
            MEMSET(misc[:, 0:16], 0.0, ["hprev", "uxprev"])

            def interleave(*gens):
                gens = list(gens)
                while gens:
                    for g in list(gens):
                        try:
                            next(g)
                        except StopIteration:
                            gens.remove(g)

            def gn_gen(h, W, bs, banks):
                (t1, k1), (t2, k2), (tmp, kt), (ob, kob), (osq, ksq) = bs["t1"], bs["t2"], bs["gt"], bs["ob"], bs["osq"]
                bm, bq = banks
                VCOPY(ob[:, :W], ps[h][:, :W], [("ps", h)], kob)
                yield
                ACT(osq[:, :W], ps[h][:, :W], AF.Square, [("ps", h)], ksq)
                yield
                MM(ps[bm][:, :W], o128, ob[:, :W], True, True, [AK("o128")] + kob, [("ps", bm)])
                MM(ps[bq][:, :W], o128, osq[:, :W], True, True, [AK("o128")] + ksq, [("ps", bq)])
                yield
                ACT(t1[:, :W], ps[bm][:, :W], AF.Square, [("ps", bm)], k1)
                yield
                TT(t1[:, :W], ps[bq][:, :W], t1[:, :W], ALU.subtract, [("ps", bq)] + k1, k1)
                yield
                ACT(t1[:, :W], t1[:, :W], AF.Ln, k1 + ["cst"], k1, bias=cst[:, 0:1])
                yield
                ACT(t1[:, :W], t1[:, :W], AF.Exp, k1, k1, scale=-0.5)
                yield
                TT(t2[:, :W], ps[bm][:, :W], t1[:, :W], ALU.mult, [("ps", bm)] + k1, k2)
                yield
                TT(tmp[:, :W], ps[h][:, :W], t1[:, :W], ALU.mult, [("ps", h)] + k1, kt)
                yield
                TT(tmp[:, :W], tmp[:, :W], t2[:, :W], ALU.subtract, kt + k2, kt)
                yield
                STT(mixT[:, h, :W], tmp[:, :W], PRM[:, P_GNG + j * 4 + h:P_GNG + j * 4 + h + 1], sgT[:, h, :W], ALU.mult, ALU.mult,
                    kt + ["PRM", AK("sgT", h)], [AK("mixT", h)])
                yield

            def rotary_pair(slot_a, slot_b, t, t1, t2, cosb, sinb, sink):
                c0, W = tiles[t]
                for h in range(4):
                    ba, bb = (0, 1) if h % 2 == 0 else (2, 3)
                    proj(slot_a, h * 128, ba, t)
                    proj(slot_b, h * 128, bb, t)
                    TT(t1[:, :W], ps[ba][:, :W], cosb[:, :W], ALU.mult, [("ps", ba), AK("cosb")], [AK("t1")])
                    TT(t2[:, :W], ps[bb][:, :W], sinb[:, :W], ALU.mult, [("ps", bb), AK("sinb")], [AK("t2")])
                    sink(h)
                    ln_drip(1)

            def retention_and_gate(t):
                c0, W = tiles[t]
                sample = (t == NT)
                nch = W // 128
                barrier()
                carve_reset(base)
                qT_f, qdT_f, kT_f = carve(4 * W), carve(4 * W), carve(4 * W)
                vtok_f = carve(max(nch, 1) * 512)
                qT = qT_f.rearrange("p (h c) -> p h c", h=4)
                qdT = qdT_f.rearrange("p (h c) -> p h c", h=4)
                kT = kT_f.rearrange("p (h c) -> p h c", h=4)
                vtok = vtok_f.rearrange("p (k c) -> p k c", k=max(nch, 1))
                t1 = carve(W, F32)
                t2 = carve(W, F32)
                cosb = carve(W, F32)
                sinb = carve(W, F32)
                PT = carve(512)
                kd = carve(512)
                ob = carve(512)
                tc0 = (c0 - PAD)
                DMAS("sp", [(cosb[:, :W], costab[:, tc0:tc0 + W]), (sinb[:, :W], sintab[:, tc0:tc0 + W])], (), [AK("cosb"), AK("sinb")], "cs")

                sq_, sqs = fill_cols(j, 0), fill_cols_swapped(j, 0)

                def qsink(h):
                    TT(t1[:, :W], t1[:, :W], t2[:, :W], ALU.add, [AK("t1"), AK("t2")], [AK("t1")])
                    if sample:
                        S.act(lambda e: e.mul(qT[:, h, :W], t1[:, :W], QSC), [AK("t1")], [AK("qT", h)])
                    else:
                        ACOPY(qT[:, h, :W], t1[:, :W], [AK("t1")], [AK("qT", h)])
                        TT(qdT[:, h, :W].rearrange("p (c i) -> p c i", i=128), t1[:, :W].rearrange("p (c i) -> p c i", i=128),
                           qdec[:, h * 128:(h + 1) * 128].unsqueeze(1).broadcast_to([128, nch, 128]), ALU.mult,
                           [AK("t1"), AK("dtab")], [AK("qdT", h)])

                rotary_pair(sq_, sqs, t, t1, t2, cosb, sinb, qsink)
                sk_, sks = fill_cols(j, 512), fill_cols_swapped(j, 512)

                def ksink(h):
                    TT(kT[:, h, :W], t1[:, :W], t2[:, :W], ALU.add, [AK("t1"), AK("t2")], [AK("kT", h)])

                rotary_pair(sk_, sks, t, t1, t2, cosb, sinb, ksink)
                sg_ = fill_cols(j, 1536)
                for h in range(4):
                    gb = 4 + h % 2
                    proj(sg_, h * 128, gb, t)
                    ACT(sgT[:, h, :W], ps[gb][:, :W], AF.Silu, [("ps", gb)], [AK("sgT", h)])
                sv_ = fill_cols(j, 1024)
                svv = slot_view(sv_)
                if not sample:
                    for c in range(nch):
                        vb = 6 + c % 2
                        for kc in range(KC):
                            MM(ps[vb][:, :], xb[:, kc, c0 + c * 128:c0 + (c + 1) * 128], svv[:, kc, :], kc == 0, kc == KC - 1,
                               [("w", sv_), ("xb", kc, t)], [("ps", vb)])
                        ACOPY(vtok[:, c, :], ps[vb][:, :], [("ps", vb)], [AK("vtok", c)])
                    for c in range(nch):
                        cc = slice(c * 128, (c + 1) * 128)
                        for h in range(4):
                            hs = slice(h * 128, (h + 1) * 128)
                            MM(ps[4][:, hs], kT[:, h, cc], qT[:, h, cc], True, True, [AK("kT", h), AK("qT", h)], [("ps", 4)])
                        TT(PT[:, :], ps[4][:, :], decayT[:, :], ALU.mult, [("ps", 4), AK("dtab")], [AK("PT")])
                        for h in range(4):
                            hs = slice(h * 128, (h + 1) * 128)
                            MM(ps[h][:, cc], vtok[:, c, hs], PT[:, hs], True, False, [AK("vtok", c), AK("PT")], [("ps", h)])
                            MM(ps[h][:, cc], Sbf[:, hs], qdT[:, h, cc], False, True, [AK("Sbf"), AK("qdT", h)], [("ps", h)])
                        p5 = ps[5][:, :].bitcast(BF16)
                        for h in range(4):
                            hs = slice(h * 128, (h + 1) * 128)
                            TR(p5[:, hs], kT[:, h, cc], identb[:], [AK("kT", h), "identb"], [("ps", 5)])
                        for h in range(4):
                            hs = slice(h * 128, (h + 1) * 128)
                            TS(kd[:, hs], p5[:, hs], kdecs[:, h:h + 1], None, ALU.mult, None, [("ps", 5), "kdecs"], [AK("kd")])
                        for h in range(4):
                            hs = slice(h * 128, (h + 1) * 128)
                            MM(ps[6][:, hs], kd[:, hs], vtok[:, c, hs], True, True, [AK("kd"), AK("vtok", c)], [("ps", 6)])
                        for h in range(4):
                            hs = slice(h * 128, (h + 1) * 128)
                            STT(Sst[:, hs], Sst[:, hs], CDEC[h], ps[6][:, hs], ALU.mult, ALU.add, [AK("Sst"), ("ps", 6)], [AK("Sst")])
                        VCOPY(Sbf[:, :], Sst[:, :], [AK("Sst")], [AK("Sbf")])
                    if t == NT - 1:
                        DMA("sp", ret_p[j].rearrange("h d v -> d h v"), Sst.rearrange("p (h v) -> p h v", h=4), [AK("Sst")], (), "rp%d" % j)
                else:
                    ktok = carve(512)
                    km = [carve(512) for _ in range(2)]
                    GB = 2
                    NSB = 3
                    S4 = [carve(GB * 512, F32).rearrange("p (b c) -> p b c", b=GB) for _ in range(NSB)]
                    Sb4 = [carve(GB * 512).rearrange("p (b c) -> p b c", b=GB) for _ in range(NSB)]
                    for kc in range(KC):
                        MM(ps[6][0:NS, :], xb[:, kc, c0:c0 + NS], svv[:, kc, :], kc == 0, kc == KC - 1, [("w", sv_), ("xb", kc, t)], [("ps", 6)])
                    ACOPY(vtok[0:NS, 0, :], ps[6][0:NS, :], [("ps", 6)], [AK("vtok", 0)])
                    p5 = ps[5][:, :].bitcast(BF16)
                    for h in range(4):
                        hs = slice(h * 128, (h + 1) * 128)
                        TR(p5[0:NS, hs], kT[:, h, 0:NS], identb[:], [AK("kT", h), "identb"], [("ps", 5)])
                    ACOPY(ktok[0:NS, :], p5[0:NS, 0:512], [("ps", 5)], [AK("ktok")])
                    def s4_load(g):
                        sbx = g % NSB
                        DMA("sp", S4[sbx].rearrange("p b (h v) -> p b h v", h=4), st_ret[j][g * GB:(g + 1) * GB].rearrange("b h d v -> d b h v"),
                            (), [AK("S4", sbx)], "s4_%d" % sbx)

                    for g in range(NSB):
                        s4_load(g)
                    for g4 in range(NS // GB):
                        sb_ = g4 % NSB
                        for bb in range(GB):
                            b = g4 * GB + bb
                            kb = b % 2
                            TS(km[kb][0:NS, :], ktok[0:NS, :], ident[0:NS, b:b + 1], None, ALU.mult, None, [AK("ktok"), "identraw"], [AK("km", kb)])
                            for h in range(4):
                                hs = slice(h * 128, (h + 1) * 128)
                                MM(ps[4][:, hs], km[kb][0:NS, hs], vtok[0:NS, 0, hs], True, True, [AK("km", kb), AK("vtok", 0)], [("ps", 4)])
                            for h in range(4):
                                hs = slice(h * 128, (h + 1) * 128)
                                STT(S4[sb_][:, bb, hs], S4[sb_][:, bb, hs], GAM[h], ps[4][:, hs], ALU.mult, ALU.add, [AK("S4", sb_), ("ps", 4)], [AK("S4", sb_)])
                            ACOPY(Sb4[sb_][:, bb, :], S4[sb_][:, bb, :], [AK("S4", sb_)], [AK("Sb4", sb_)])
                            for h in range(4):
                                hs = slice(h * 128, (h + 1) * 128)
                                MM(ps[h][:, b:b + 1], Sb4[sb_][:, bb, hs], qT[:, h, b:b + 1], True, True, [AK("Sb4", sb_), AK("qT", h)], [("ps", h)])
                        DMA("sp", ret_s[j][g4 * GB:(g4 + 1) * GB].rearrange("b h d v -> d b h v"), S4[sb_].rearrange("p b (h v) -> p b h v", h=4),
                            [AK("S4", sb_)], (), "s4_%d" % sb_)
                        if g4 + NSB < NS // GB:
                            s4_load(g4 + NSB)
                set0 = dict(t1=(t1, [AK("t1")]), t2=(t2, [AK("t2")]), gt=(cosb, [AK("cosb")]), ob=(ob, [AK("ob")]), osq=(PT, [AK("PT")]))
                if sample:
                    for h in range(4):
                        for _ in gn_gen(h, W, set0, (7, 5)):
                            pass
                else:
                    half = 2 * W
                    set1 = dict(t1=(qT_f[:, 0:half].bitcast(F32), [AK("qT", 0), AK("qT", 1)]),
                                t2=(qT_f[:, half:2 * half].bitcast(F32), [AK("qT", 2), AK("qT", 3)]),
                                gt=(qdT_f[:, 0:half].bitcast(F32), [AK("qdT", 0), AK("qdT", 1)]),
                                ob=(qdT_f[:, half:half + W], [AK("qdT", 2)]), osq=(qdT_f[:, half + W:half + 2 * W], [AK("qdT", 3)]))
                    set2 = dict(t1=(kT_f[:, 0:half].bitcast(F32), [AK("kT", 0), AK("kT", 1)]),
                                t2=(kT_f[:, half:2 * half].bitcast(F32), [AK("kT", 2), AK("kT", 3)]),
                                gt=(vtok_f[:, 0:half].bitcast(F32), [AK("vtok", 0), AK("vtok", 1)]),
                                ob=(vtok_f[:, half:half + W], [AK("vtok", 2)]), osq=(vtok_f[:, half + W:half + 2 * W], [AK("vtok", 3)]))
                    interleave(gn_gen(0, W, set1, (7, 5)), gn_gen(1, W, set2, (6, 4)))
                    interleave(gn_gen(2, W, set0, (7, 5)), gn_gen(3, W, set1, (6, 4)))

            def lru(t):
                c0, W = tiles[t]
                sample = (t == NT)
                barrier()
                carve_reset(base)
                NUX = 4 if sample else 2
                uxe = carve(NUX * (W + 8), F32).rearrange("p (n c) -> p n c", n=NUX)
                uc2 = [carve(W, F32) for _ in range(2)]
                ucb2 = [carve(W) for _ in range(2)]
                aa2 = [carve(W, F32) for _ in range(2)]
                ei2 = [carve(W, F32) for _ in range(2)]
                a2 = carve(W, F32)
                iu = carve(W, F32)
                hh = carve(W, F32)
                sq = carve(W, F32)
                if sample:
                    cbuf = carve(4 * 48, F32).rearrange("p (n b i) -> p n b i", n=4, i=3)
                    h0T = carve(4 * NS, F32).rearrange("p (n b) -> p n b", n=4)
                    hsT = carve(4 * NS, F32).rearrange("p (n b) -> p n b", n=4)
                    tokb = carve(512, F32)
                    DMAS("sp", [(stage[0:48, 0, 0:512], st_conv[j].rearrange("b i c -> (b i) c")), (stage[0:NS, 1, 0:512], st_h[j])],
                         (), [("stage", 0), ("stage", 1)], "stgL")
                    for n in range(4):
                        TR(ps[6][:, n * 48:(n + 1) * 48], stage[0:48, 0, n * 128:(n + 1) * 128], ident[0:48, 0:48], [("stage", 0), "identraw"], [("ps", 6)])
                        TR(ps[7][:, n * NS:(n + 1) * NS], stage[0:NS, 1, n * 128:(n + 1) * 128], ident[0:NS, 0:NS], [("stage", 1), "identraw"], [("ps", 7)])
                    ACOPY(cbuf.rearrange("p n b i -> p (n b i)"), ps[6][:, 0:192], [("ps", 6)], [AK("cbuf")])
                    ACOPY(h0T.rearrange("p n b -> p (n b)"), ps[7][:, 0:4 * NS], [("ps", 7)], [AK("h0T")])
                    DMA("sp", conv_s[j][:, 0:2, :], st_conv[j][:, 1:3, :], (), (), "ccp%d" % j)
                if t == 0:
                    DMAS("pool", [(wab.rearrange("p (n d) -> p n d", n=4), lru_wa[j].rearrange("n c d -> c n d")),
                                  (wib.rearrange("p (n d) -> p n d", n=4), lru_wi[j].rearrange("n c d -> c n d"))], (), [AK("wab")], "mw%d" % j)
                sux, sug = fill_cols(j, 2048), fill_cols(j, 2560)

                def early(n):
                    p = n % 2
                    xi = n % NUX
                    uc, ucb, aa, ei = uc2[p], ucb2[p], aa2[p], ei2[p]
                    ns = slice(n * 128, (n + 1) * 128)
                    ub = n % 2
                    proj(sux, n * 128, ub, t)
                    yield
                    ACOPY(uxe[:, xi, 3:3 + W], ps[ub][:, :W], [("ps", ub)], [AK("uxe", xi)])
                    w = lambda i: PRM[:, P_CW + (j * 4 + i) * 4 + n:P_CW + (j * 4 + i) * 4 + n + 1]
                    cbias = PRM[:, P_CB + j * 4 + n:P_CB + j * 4 + n + 1]
                    yield
                    TS(uc[:, :W], uxe[:, xi, 3:3 + W], w(3), cbias, ALU.mult, ALU.add, [AK("uxe", xi), "PRM"], [AK("uc", p)])
                    if not sample:
                        ACOPY(uxe[:, xi, 0:3], uxprev[:, n, :], ["uxprev"], [AK("uxe", xi)])
                        yield
                        for i in (2, 1, 0):
                            STT(uc[:, :W], uxe[:, xi, i:i + W], w(i), uc[:, :W], ALU.mult, ALU.add, [AK("uxe", xi), AK("uc", p), "PRM"], [AK("uc", p)])
                            yield
                        ACOPY(uxprev[:, n, :], uxe[:, xi, W:W + 3], [AK("uxe", xi)], ["uxprev"])
                    else:
                        for i in (2, 1, 0):
                            STT(uc[:, :W], cbuf[:, n, :, i], w(i), uc[:, :W], ALU.mult, ALU.add, [AK("cbuf"), AK("uc", p), "PRM"], [AK("uc", p)])
                            yield
                    VCOPY(ucb[:, :W], uc[:, :W], [AK("uc", p)], [AK("ucb", p)])
                    yield
                    MM(ps[2][:, :W], wab[:, ns], ucb[:, :W], True, True, [AK("wab"), AK("ucb", p)], [("ps", 2)])
                    MM(ps[3][:, :W], wib[:, ns], ucb[:, :W], True, True, [AK("wab"), AK("ucb", p)], [("ps", 3)])
                    yield
                    nba = DRV[:, D_NBA + j * 4 + n:D_NBA + j * 4 + n + 1]
                    nbi = DRV[:, D_NBI + j * 4 + n:D_NBI + j * 4 + n + 1]
                    c1 = DRV[:, D_C1 + j * 4 + n:D_C1 + j * 4 + n + 1]
                    ACT(aa[:, :W], ps[2][:, :W], AF.Exp, [("ps", 2), "DRV"], [AK("aa", p)], bias=nba, scale=-1.0)
                    yield
                    ACT(ei[:, :W], ps[3][:, :W], AF.Exp, [("ps", 3), "DRV"], [AK("ei", p)], bias=nbi, scale=-1.0)
                    yield
                    ACT(aa[:, :W], aa[:, :W], AF.Ln, [AK("aa", p), "cst"], [AK("aa", p)], bias=cst[:, 1:2])
                    yield
                    ACT(ei[:, :W], ei[:, :W], AF.Ln, [AK("ei", p), "cst"], [AK("ei", p)], bias=cst[:, 1:2])
                    yield
                    ACT(aa[:, :W], aa[:, :W], AF.Exp, [AK("aa", p)], [AK("aa", p)], scale=-1.0)
                    yield
                    ACT(ei[:, :W], ei[:, :W], AF.Exp, [AK("ei", p)], [AK("ei", p)], scale=-1.0)
                    yield
                    ACT(aa[:, :W], aa[:, :W], AF.Exp, [AK("aa", p), "DRV"], [AK("aa", p)], scale=c1)
                    yield

                def late(n):
                    p = n % 2
                    uc, aa, ei = uc2[p], aa2[p], ei2[p]
                    ACT(a2[:, :W], aa[:, :W], AF.Square, [AK("aa", p)], [AK("a2")])
                    yield
                    gb = 4 + n % 2
                    proj(sug, n * 128, gb, t)
                    yield
                    ACT(a2[:, :W], a2[:, :W], AF.Ln, [AK("a2"), "cst"], [AK("a2")], bias=cst[:, 3:4], scale=-1.0)
                    yield
                    ACT(sq[:, :W], ps[gb][:, :W], AF.Square, [("ps", gb)], [AK("sq")])
                    yield
                    ACT(a2[:, :W], a2[:, :W], AF.Exp, [AK("a2")], [AK("a2")], scale=0.5)
                    TT(iu[:, :W], ei[:, :W], uc[:, :W], ALU.mult, [AK("ei", p), AK("uc", p)], [AK("iu")])
                    yield
                    TS(sq[:, :W], sq[:, :W], 0.044715, 1.0, ALU.mult, ALU.add, [AK("sq")], [AK("sq")])
                    yield
                    TT(iu[:, :W], iu[:, :W], a2[:, :W], ALU.mult, [AK("iu"), AK("a2")], [AK("iu")])
                    yield
                    TT(sq[:, :W], sq[:, :W], ps[gb][:, :W], ALU.mult, [AK("sq"), ("ps", gb)], [AK("sq")])
                    yield
                    if not sample:
                        S.dve(lambda e, n=n: e.tensor_tensor_scan(hh[:, :W], aa[:, :W], iu[:, :W], hprev[:, n:n + 1], ALU.mult, ALU.add),
                              [AK("aa", p), AK("iu"), "hprev"], [AK("hh")])
                        ACOPY(hprev[:, n:n + 1], hh[:, W - 1:W], [AK("hh")], ["hprev"])
                    else:
                        TT(hh[:, :W], aa[:, :W], h0T[:, n, :], ALU.mult, [AK("aa", p), AK("h0T")], [AK("hh")])
                        TT(hh[:, :W], hh[:, :W], iu[:, :W], ALU.add, [AK("hh"), AK("iu")], [AK("hh")])
                        ACOPY(hsT[:, n, :], hh[:, :W], [AK("hh")], [AK("hsT")])
                    yield
                    ACT(sq[:, :W], sq[:, :W], AF.Exp, [AK("sq")], [AK("sq")], scale=-2.0 * GELU_C)
                    yield
                    ACT(sq[:, :W], sq[:, :W], AF.Ln, [AK("sq"), "cst"], [AK("sq")], bias=cst[:, 1:2])
                    yield
                    ACT(sq[:, :W], sq[:, :W], AF.Exp, [AK("sq")], [AK("sq")], scale=-1.0)
                    yield
                    TT(sq[:, :W], sq[:, :W], ps[gb][:, :W], ALU.mult, [AK("sq"), ("ps", gb)], [AK("sq")])
                    yield
                    TT(mixT[:, 4 + n, :W], sq[:, :W], hh[:, :W], ALU.mult, [AK("sq"), AK("hh")], [AK("mixT", 4 + n)])
                    yield

                interleave(early(0))
                for n in range(4):
                    if n + 1 < 4:
                        interleave(late(n), early(n + 1))
                    else:
                        interleave(late(n))
                if sample:
                    for n in range(4):
                        TR(ps[6][0:NS, n * 128:(n + 1) * 128], hsT[:, n, :], ident[:], [AK("hsT"), "identraw"], [("ps", 6)])
                        TR(ps[7][0:NS, n * 128:(n + 1) * 128], uxe[:, n, 3:3 + NS], ident[:], [AK("uxe", n), "identraw"], [("ps", 7)])
                    ACOPY(tokb[0:NS, :], ps[6][0:NS, :], [("ps", 6)], [AK("tokb")])
                    DMA("sp", h_s[j], tokb[0:NS, :], [AK("tokb")], (), "tokb")
                    ACOPY(tokb[0:NS, :], ps[7][0:NS, :], [("ps", 7)], [AK("tokb")])
                    DMA("sp", conv_s[j][:, 2, :], tokb[0:NS, :], [AK("tokb")], (), "tokb")
                elif t == NT - 1:
                    tokp = carve(128, F32)
                    TR(ps[6][0:16, 0:128], misc[:, 0:16], ident[:], ["hprev", "uxprev", "identraw"], [("ps", 6)])
                    ACOPY(tokp[0:16, :], ps[6][0:16, 0:128], [("ps", 6)], [AK("tokp")])
                    prs = [(h_p[j].rearrange("(n p) -> n p", p=128), tokp[0:4, :])]
                    for n in range(4):
                        prs.append((conv_p[j][:, n * 128:(n + 1) * 128], tokp[4 + 3 * n:7 + 3 * n, :]))
                    DMAS("sp", prs, [AK("tokp")], (), "tokp%d" % j)

            def outproj_ln(t):
                c0, W = tiles[t]
                slots = []
                for half in range(2):
                    def pairs(slotv, half=half):
                        return [(slotv.rearrange("p (c m) -> p c m", c=4), w_out[j, half * 512:(half + 1) * 512, :].rearrange("(c p) m -> p c m", p=128))]
                    slots.append(ring_fill(pairs))
                for m in range(KC):
                    yb = 4 + cnt["y"] % 2
                    cnt["y"] += 1
                    for c in range(8):
                        slot = slots[c // 4]
                        wv = ring[:, slot, (c % 4) * 1024 + m * 128:(c % 4) * 1024 + (m + 1) * 128]
                        MM(ps[yb][:, :W], wv, mixT[:, c, :W], c == 0, c == 7, [("w", slot), AK("mixT", c)], [("ps", yb)])
                    resid_evac(t, m, yb, "first")
                    ln_prep(t, m)
                    if m > 0:
                        ln_stats(t, m - 1)
                ln_stats(t, KC - 1)
                ln_finalize(t, l, 1, defer=True)

            for t in range(NTL):
                ln_par[0] = 0
                retention_and_gate(t)
                lru(t)
                ln_par[0] = 0
                outproj_ln(t)
            ln_flush()

        for l in range(DEPTH):
            ffn(l, 0, 0)
            if l % 2 == 1 and cfg.mixC:
                pool_mixer(l)
            if l % 2 == 0 and cfg.mixA:
                mixer_a(l)
            if cfg.mixA or cfg.mixC:
                barrier()
            ffn(l, 1, 2)

        for b in range(nblk):
            sk = b % 2
            t = b // 4
            c = PAD + b * 128
            for half in range(2):
                bank = 2 * sk + half
                for q in range(4):
                    kc = half * 4 + q
                    TR(ps[bank][:, q * 128:(q + 1) * 128], xT[:, kc, c:c + 128], ident[:], [("x", kc, t), "identraw"], [("ps", bank)])
                ACOPY(stage[:, sk, half * 512:(half + 1) * 512], ps[bank][:, :], [("ps", bank)], [("stage", sk)])
            DMA("sp", y_p[b * 128:(b + 1) * 128, :], stage[:, sk, :], [("stage", sk)], (), "stg%d" % sk)
        for half in range(2 if 'sout' not in SKIP else 0):
            for q in range(4):
                kc = half * 4 + q
                TR(ps[half][0:NS, q * 128:(q + 1) * 128], xT[:, kc, CS:CS + NS], ident[:], [("x", kc, NT), "identraw"], [("ps", half)])
            ACOPY(stage[0:NS, 0, half * 512:(half + 1) * 512], ps[half][0:NS, :], [("ps", half)], [("stage", 0)])
        if 'sout' not in SKIP:
            DMA("sp", y_s.get(), stage[0:NS, 0, :], [("stage", 0)], (), "stg0")

        S.finalize_and_emit()
    return nc


def host_tables(SEQ):
    f32 = np.float32
    half = 64
    inv = (f32(10000.0) ** (-(np.arange(half, dtype=f32)) / f32(half))).astype(f32)
    pos = np.concatenate([np.arange(SEQ, dtype=f32), np.full((NS,), PAST, f32)])
    ang = (pos[:, None] * inv[None, :]).astype(f32)
    cos = np.cos(ang).astype(f32).T
    sin = np.sin(ang).astype(f32).T
    costab = np.concatenate([cos, cos], axis=0)
    sintab = np.concatenate([-sin, sin], axis=0)
    lg = np.log1p(-np.exp2(-5.0 - np.arange(4, dtype=f32))).astype(f32)
    idx = np.arange(128, dtype=f32)
    sc = f32(128 ** -0.5)
    diff = idx[None, :] - idx[:, None]
    decayT = np.where(diff[:, None, :] >= 0, np.exp(lg[None, :, None] * np.maximum(diff[:, None, :], 0.0)), 0.0) * sc
    qdec = np.broadcast_to((np.exp(lg[:, None] * (idx[None, :] + 1.0)) * sc)[None], (128, 4, 128))
    kdec = np.broadcast_to(np.exp(lg[None, :, None] * (127.0 - idx[:, None, None])), (128, 4, 128))
    cdec = np.broadcast_to(np.exp(lg * 128.0)[None, :, None], (128, 4, 128))
    gam = np.broadcast_to(np.exp(lg)[None, :, None], (128, 4, 128))
    selp = np.zeros((240, 4, NS), f32)
    for g, w in enumerate((2, 4, 8, 16)):
        for b in range(NS):
            for i in range(15 - (w - 1), 15):
                selp[b * 15 + i, g, b] = 1.0
    selp = selp.reshape(2, 120, 4, NS).transpose(1, 0, 2, 3)
    icnt = np.zeros((128, 4, 16), f32)
    for g, w in enumerate((2, 4, 8, 16)):
        icnt[:, g, :] = 1.0 / np.minimum(float(w), np.arange(16, dtype=f32) + 1.0)
    kdecs = np.exp(lg[None, :] * (127.0 - idx[:, None])).astype(f32)
    c = lambda a: np.ascontiguousarray(a.reshape(128, -1).astype(f32))
    return dict(costab=np.ascontiguousarray(costab), sintab=np.ascontiguousarray(sintab), decayT=c(decayT), qdec=c(qdec),
                kdec=c(kdec), kdecs=np.ascontiguousarray(kdecs), cdec=c(cdec), gamtab=c(gam), selp=np.ascontiguousarray(selp.astype(f32)), icnt=icnt,
                ident=np.eye(128, dtype=f32))


def make_params(inp):
    f32 = np.float32
    rows = [inp["ln_g"].reshape(-1, 128), inp["ln_b"].reshape(-1, 128), inp["ret_gn_g"].reshape(-1, 128),
            inp["lru_conv_w"].reshape(-1, 128), inp["lru_conv_b"].reshape(-1, 128), inp["lru_ba"].reshape(-1, 128),
            inp["lru_bi"].reshape(-1, 128), inp["lru_lambda"].reshape(-1, 128), inp["pool_b"].reshape(-1, 128),
            inp["pool_scale"].reshape(-1, 128)]
    offs = [P_LNG, P_LNB, P_GNG, P_CW, P_CB, P_BA, P_BI, P_LAM, P_PB, P_PS]
    out = np.zeros((P_ROWS, 128), f32)
    for o, r in zip(offs, rows):
        out[o:o + r.shape[0]] = r
    return out


def make_in_maps(cfg, inp):
    SEQ = cfg.SEQ
    tabs = host_tables(SEQ)
    prm = make_params(inp)
    A = np.ascontiguousarray
    NWL = max(cfg.DEPTH, 1) if getattr(cfg, "tinyw", False) else 4
    shared = dict(wg=A(inp["w_ffn_gate"][:NWL]), wu=A(inp["w_ffn_up"][:NWL]), wd=A(inp["w_ffn_down"][:NWL]), w_in=A(inp["w_mix_in"]),
                  w_out=A(inp["w_mix_out"]), w_insw=A(inp["w_mix_in"][:, :, :1024].reshape(2, D, 8, 2, 64)[:, :, :, ::-1, :].reshape(2, D, 1024)), lru_wa=A(inp["lru_wa"]), lru_wi=A(inp["lru_wi"]), pool_w=A(inp["pool_w"]),
                  params=prm, **tabs)
    maps = []
    for c in range(8):
        sl = slice(c * NS, (c + 1) * NS)
        m = dict(shared)
        m["xp"] = A(inp["x_prompt"][c, :SEQ])
        m["xs"] = A(inp["x_sample"][sl, 0])
        m["st_ret"] = A(inp["state_ret"][:, sl])
        m["st_h"] = A(inp["state_lru_h"][:, sl])
        m["st_conv"] = A(inp["state_lru_conv"][:, sl])
        m["st_pool"] = A(inp["state_pool"][:, sl])
        maps.append(m)
    return maps


_CACHE = {}


def run(cfg, inp):
    key = (cfg.SEQ, cfg.DEPTH, cfg.mixA, cfg.mixC)
    if key not in _CACHE:
        _CACHE[key] = build_program(cfg)
    nc = _CACHE[key]
    maps = make_in_maps(cfg, inp)
    used = set()
    for alloc in nc.allocations:
        if isinstance(alloc, mybir.MemoryLocationSet) and alloc.kind == "ExternalInput":
            used.add(alloc.memorylocations[0].name)
    maps = [{k: v for k, v in m.items() if k in used} for m in maps]
    res = run_bass_kernel_spmd(nc, maps, core_ids=list(range(8)))
    R = res.results
    shp = dict(y_p=(cfg.SEQ, D), y_s=(NS, D), ret_p=(2, 4, 128, 128), h_p=(2, 512), conv_p=(2, 3, 512), pool_p=(2, 15, D),
               ret_s=(2, NS, 4, 128, 128), h_s=(2, NS, 512), conv_s=(2, NS, 3, 512), pool_s=(2, NS, 15, D))
    for r in R:
        for k, sh in shp.items():
            if k not in r:
                r[k] = np.zeros(sh, np.float32)
    cat = lambda k, ax: np.concatenate([r[k] for r in R], axis=ax)
    y_p = np.stack([r["y_p"] for r in R], 0)
    y_s = cat("y_s", 0)[:, None, :]
    ret_p = np.stack([r["ret_p"] for r in R], 1)
    h_p = np.stack([r["h_p"] for r in R], 1)
    conv_p = np.stack([r["conv_p"] for r in R], 1)
    pool_p = np.stack([r["pool_p"] for r in R], 1)
    ret_s = cat("ret_s", 1)
    h_s = cat("h_s", 1)
    conv_s = cat("conv_s", 1)
    pool_s = cat("pool_s", 1)
    return (y_p, y_s, ret_p, h_p, conv_p, pool_p, ret_s, h_s, conv_s, pool_s)


def kernel(**inputs):
    inp = {k: np.asarray(v) for k, v in inputs.items()}
    outs = run(Cfg(), inp)
    return tuple(np.ascontiguousarray(o.astype(np.float32)) for o in outs)
```

```python
import contextlib
import math
import os
SKIP = set(os.environ.get('KSKIP', '').split(','))
import numpy as np
import concourse.bass as bass
import concourse.mybir as mybir
from concourse.bass_utils import run_bass_kernel_spmd

F32 = mybir.dt.float32
BF16 = mybir.dt.bfloat16
AF = mybir.ActivationFunctionType
ALU = mybir.AluOpType

ENGS = ("pe", "act", "dve", "pool", "sp")


class Op:
    __slots__ = ("eng", "idx", "emit", "deps", "inc", "dma", "dma_val", "ndma", "tag")


class Sched:
    def __init__(self, nc):
        self.nc = nc
        self.streams = {e: [] for e in ENGS}
        self.lastw = {}
        self.readers = {}
        self.dma_cnt = {}
        self.all_ops = []

    def add(self, eng, emit, reads=(), writes=(), dma=None, ndma=1, tag=None):
        op = Op()
        op.eng = eng
        op.emit = emit
        op.inc = False
        op.dma = dma
        op.ndma = ndma
        op.tag = tag
        writes = list(writes) + [k for k in reads if isinstance(k, tuple) and k[0] == "ps" and k not in writes]
        reads = [k for k in reads if not (isinstance(k, tuple) and k[0] == "ps")]
        for k in reads + writes:
            if isinstance(k, tuple) and k[0] == "A":
                reads.append("EPOCH")
                break
        deps = set()
        for k in reads:
            w = self.lastw.get(k)
            if w is not None:
                deps.add(w)
        for k in writes:
            w = self.lastw.get(k)
            if w is not None:
                deps.add(w)
            rd = self.readers.get(k)
            if rd:
                deps.update(rd.values())
        rk = ("d", len(self.all_ops)) if dma is not None else eng
        for k in reads:
            self.readers.setdefault(k, {})[rk] = op
        for k in writes:
            self.lastw[k] = op
            self.readers[k] = {}
        if dma is not None:
            c = self.dma_cnt.get(dma, 0) + ndma
            self.dma_cnt[dma] = c
            op.dma_val = 16 * c
        elif eng == "pe":
            deps = {d for d in deps if not (d.eng == "pe" and d.dma is None)}
        deps.discard(op)
        op.deps = deps
        st = self.streams[eng]
        st.append(op)
        op.idx = len(st)
        self.all_ops.append(op)
        return op

    def pe(self, emit, reads=(), writes=(), **kw):
        return self.add("pe", emit, reads, writes, **kw)

    def act(self, emit, reads=(), writes=(), **kw):
        return self.add("act", emit, reads, writes, **kw)

    def dve(self, emit, reads=(), writes=(), **kw):
        return self.add("dve", emit, reads, writes, **kw)

    def finalize_and_emit(self, final_wait_eng="sp"):
        nc = self.nc
        for op in self.all_ops:
            for d in op.deps:
                if d.dma is None:
                    d.inc = True
        with contextlib.ExitStack() as es:
            esem = {e: es.enter_context(nc.semaphore("s_" + e)) for e in ENGS if e != "sp"}
            dsem = {k: es.enter_context(nc.semaphore("d_" + str(k))) for k in self.dma_cnt}
            last_vals = {}
            val = {}
            for e in ENGS:
                if e == "sp":
                    continue
                lst = [op for op in self.streams[e] if op.dma is None]
                if lst:
                    lst[-1].inc = True
                c = 0
                for op in self.streams[e]:
                    if op.dma is None and op.inc:
                        c += 1
                    val[op] = c
                if lst:
                    last_vals[e] = val[lst[-1]]
            engobj = {"pe": "tensor", "act": "scalar", "dve": "vector", "pool": "gpsimd", "sp": "sync"}
            blk = es.enter_context(nc.Block())

            def make(e):
                def body(eng):
                    waited = {}
                    for op in self.streams[e]:
                        need = {}
                        for d in op.deps:
                            if d.dma is not None:
                                s = dsem[d.dma]
                                v = d.dma_val
                            else:
                                s = esem[d.eng]
                                v = val[d]
                            key = id(s)
                            if key not in need or need[key][1] < v:
                                need[key] = (s, v)
                        for key, (s, v) in need.items():
                            if waited.get(key, 0) >= v:
                                continue
                            eng.wait_ge(s, v)
                            waited[key] = v
                        r = op.emit(eng)
                        if op.dma is not None:
                            if not isinstance(r, (list, tuple)):
                                r = [r]
                            assert len(r) == op.ndma, (len(r), op.ndma, op.tag)
                            for ins in r:
                                ins.then_inc(dsem[op.dma], 16)
                        elif op.inc:
                            r.then_inc(esem[e], 1)
                    if e == final_wait_eng:
                        for k, c in self.dma_cnt.items():
                            eng.wait_ge(dsem[k], 16 * c)
                        for e2, v in last_vals.items():
                            eng.wait_ge(esem[e2], v)

                return body

            for e in ENGS:
                getattr(blk, engobj[e])(make(e))


D = 1024
KC = 8
DFF = 2816
NFC = 22
ALPHA = (2.0 * 4) ** 0.25
LN_EPS = 1e-5
PAD = 16
NS = 16
PAST = 16384
HALVES = ((0, 12), (12, 22))
GELU_C = math.sqrt(2.0 / math.pi)

P_LNG = 0
P_LNB = 96
P_GNG = 192
P_CW = 200
P_CB = 232
P_BA = 240
P_BI = 248
P_LAM = 256
P_PB = 264
P_PS = 280
P_ROWS = 384


class Cfg:
    def __init__(self, SEQ=2048, DEPTH=4, mixers=True, mixA=None, mixC=None):
        self.SEQ = SEQ
        self.DEPTH = DEPTH
        self.mixers = mixers
        self.mixA = mixers if mixA is None else mixA
        self.mixC = mixers if mixC is None else mixC


def build_program(cfg):
    SEQ = cfg.SEQ
    DEPTH = cfg.DEPTH
    NT = SEQ // 512
    TTP = PAD + SEQ + NS
    CS = PAD + SEQ
    tiles = [(PAD + 512 * i, 512) for i in range(NT)] + [(CS, NS)]
    NTL = len(tiles)
    NA = (DEPTH + 1) // 2
    NCL = DEPTH // 2

    nc = bass.Bass("TRN2", target_bir_lowering=False)

    class _Lazy:
        def __init__(self, name, shape, kind="ExternalInput"):
            self.name, self.shape, self.ap_, self.kind = name, shape, None, kind

        def get(self):
            if self.ap_ is None:
                self.ap_ = nc.dram_tensor(self.name, list(self.shape), F32, kind=self.kind).ap()
            return self.ap_

        def __getitem__(self, k):
            return self.get()[k]

        def rearrange(self, *a, **kw):
            return self.get().rearrange(*a, **kw)

    def din(name, shape, dtype=F32):
        return _Lazy(name, shape)

    def dout(name, shape, dtype=F32):
        return _Lazy(name, shape, "ExternalOutput")

    xp = din("xp", [SEQ, D])
    xs = din("xs", [NS, D])
    st_ret = din("st_ret", [2, NS, 4, 128, 128])
    st_h = din("st_h", [2, NS, 512])
    st_conv = din("st_conv", [2, NS, 3, 512])
    st_pool = din("st_pool", [2, NS, 15, D])
    NWL = max(DEPTH, 1) if getattr(cfg, "tinyw", False) else 4
    wg = din("wg", [NWL, 2, D, DFF])
    wu = din("wu", [NWL, 2, D, DFF])
    wd = din("wd", [NWL, 2, DFF, D])
    w_in = din("w_in", [2, D, 3072])
    w_out = din("w_out", [2, D, D])
    w_insw = din("w_insw", [2, D, 1024])
    lru_wa = din("lru_wa", [2, 4, 128, 128])
    lru_wi = din("lru_wi", [2, 4, 128, 128])
    pool_w = din("pool_w", [2, 4, 256, 256])
    params = din("params", [P_ROWS, 128])
    ident_d = din("ident", [128, 128])
    costab = din("costab", [128, SEQ + NS])
    sintab = din("sintab", [128, SEQ + NS])
    decay_d = din("decayT", [128, 512])
    qdec_d = din("qdec", [128, 512])
    kdec_d = din("kdec", [128, 512])
    cdec_d = din("cdec", [128, 512])
    gam_d = din("gamtab", [128, 512])
    kdecs_d = din("kdecs", [128, 4])
    selp_d = din("selp", [120, 2, 4, NS])
    icnt_d = din("icnt", [128, 4, 16])

    y_p = dout("y_p", [SEQ, D])
    y_s = dout("y_s", [NS, D])
    ret_p = dout("ret_p", [2, 4, 128, 128])
    h_p = dout("h_p", [2, 512])
    conv_p = dout("conv_p", [2, 3, 512])
    pool_p = dout("pool_p", [2, 15, D])
    ret_s = dout("ret_s", [2, NS, 4, 128, 128])
    h_s = dout("h_s", [2, NS, 512])
    conv_s = dout("conv_s", [2, NS, 3, 512])
    pool_s = dout("pool_s", [2, NS, 15, D])

    with contextlib.ExitStack() as es:
        def sb(name, shape, dtype):
            return es.enter_context(nc.sbuf_tensor("sb_" + name, list(shape), dtype))

        xT = sb("xT", [128, KC, TTP], F32)
        xb = sb("xb", [128, KC, TTP], BF16)
        ARENA_COLS = max(12 * TTP, 24960)
        arena = sb("arena", [128, ARENA_COLS], BF16)
        ring = sb("ring", [128, 4, 4096], BF16)
        stage = sb("stage", [128, 2, D], F32)
        sgb = sb("sgb", [128, 2, 512], F32)
        zsq = sb("zsq", [128, 2, 512], BF16)
        lnA = sb("lnA", [128, 512], F32)
        lnB = sb("lnB", [128, 512], F32)
        lnC = sb("lnC", [128, 512], F32)
        lnT = sb("lnT", [128, 2, 512], F32)
        PRM = sb("PRM", [128, P_ROWS], F32)
        DRV = sb("DRV", [128, 64], F32)
        ident = sb("ident", [128, 128], F32)
        identb = sb("identb", [128, 128], BF16)
        onesb = sb("onesb", [128, 128], BF16)
        cst = sb("cst", [128, 8], F32)
        ps = [es.enter_context(nc.psum_tensor("ps%d" % i, [128, 512], F32)) for i in range(8)]

        S = Sched(nc)

        def MM(out, lhsT, rhs, start, stop, r, w):
            S.pe(lambda e: e.matmul(out, lhsT, rhs, start=start, stop=stop), r, w)

        def TR(out, in_, idn, r, w):
            S.pe(lambda e: e.transpose(out, in_, idn), r, w)

        def ACT(out, in_, func, r, w, bias=None, scale=None):
            kw = {}
            if bias is not None:
                kw["bias"] = bias
            if scale is not None:
                kw["scale"] = scale
            S.act(lambda e: e.activation(out, in_, func, **kw), r, w)

        def ACOPY(out, in_, r, w):
            S.act(lambda e: e.copy(out, in_), r, w)

        def VCOPY(out, in_, r, w):
            S.dve(lambda e: e.tensor_copy(out, in_), r, w)

        def TT(out, a, b, op, r, w):
            S.dve(lambda e: e.tensor_tensor(out, a, b, op), r, w)

        def PTT(out, a, b, op, r, w):
            S.add("pool", lambda e: e.tensor_tensor(out, a, b, op), r, w)

        def PCOPY(out, in_, r, w):
            S.add("pool", lambda e: e.tensor_copy(out, in_), r, w)

        def TS(out, a, s1, s2, op0, op1, r, w):
            if op1 is None:
                S.dve(lambda e: e.tensor_scalar(out, a, s1, None, op0), r, w)
            else:
                S.dve(lambda e: e.tensor_scalar(out, a, s1, s2, op0, op1), r, w)

        def STT(out, a, sc, b, op0, op1, r, w):
            S.dve(lambda e: e.scalar_tensor_tensor(out, a, sc, b, op0, op1), r, w)

        def MEMSET(out, v, w, eng="dve"):
            S.add(eng, lambda e: e.memset(out, v), (), w)

        def DMA(eng, out, in_, r, w, key):
            S.add(eng, lambda e: e.dma_start(out=out, in_=in_), r, w, dma=key)

        def DMAS(eng, pairs, r, w, key):
            pairs = list(pairs)
            S.add(eng, lambda e: [e.dma_start(out=o, in_=i) for (o, i) in pairs], r, w, dma=key, ndma=len(pairs))

        ring_n = [0]

        def ring_fill(pairs_fn):
            slot = ring_n[0] % 4
            ring_n[0] += 1
            pairs = pairs_fn(ring[:, slot, :])
            DMAS("pool", pairs, (), [("w", slot)], "w%d" % slot)
            return slot

        DMAS("sp", [(ident[:], ident_d.get())], (), ["identraw"], "c0")
        if 'ms1' not in SKIP:
            MEMSET(cst[:, 0:1], LN_EPS, ["cst"])
            MEMSET(cst[:, 1:2], 1.0, ["cst"])
            MEMSET(cst[:, 2:3], 1e-20, ["cst"])
            MEMSET(cst[:, 3:4], 1.0 + 1e-12, ["cst"])
        if 'ms2' not in SKIP:
            MEMSET(onesb[:], 1.0 / D, ["onesb"])
        if 'ms3' not in SKIP:
            MEMSET(xT[:, :, 0:PAD], 0.0, [("x", k, 0) for k in range(KC)])
        if 'ms4' not in SKIP:
            MEMSET(xb[:, :, 0:PAD], 0.0, [("xb", k, 0) for k in range(KC)])
        if 'idb' not in SKIP:
            ACOPY(identb[:], ident[:], ["identraw"], ["identb"])
        for g in range(3 if 'prm' not in SKIP else 0):
            DMA("sp", stage[:, 0, g * 128:(g + 1) * 128], params[g * 128:(g + 1) * 128, :], (), [("stage", 0)], "stg0")
        for g in range(3 if 'prm' not in SKIP else 0):
            TR(ps[0][:, g * 128:(g + 1) * 128], stage[:, 0, g * 128:(g + 1) * 128], ident[:], [("stage", 0), "identraw"], [("ps", 0)])
        if 'prm' not in SKIP:
            ACOPY(PRM[:, 0:384], ps[0][:, 0:384], [("ps", 0)], ["PRM"])
        D_NBA, D_NBI, D_C1, D_PBS = 0, 8, 16, 24
        if 'drv' not in SKIP:
            TS(DRV[:, D_NBA:D_NBA + 8], PRM[:, P_BA:P_BA + 8], -1.0, None, ALU.mult, None, ["PRM"], ["DRV"])
            TS(DRV[:, D_NBI:D_NBI + 8], PRM[:, P_BI:P_BI + 8], -1.0, None, ALU.mult, None, ["PRM"], ["DRV"])
            ACT(DRV[:, 40:48], PRM[:, P_LAM:P_LAM + 8], AF.Exp, ["PRM", "DRV"], ["DRV"], scale=-1.0)
            ACT(DRV[:, 40:48], DRV[:, 40:48], AF.Ln, ["DRV", "cst"], ["DRV"], bias=cst[:, 1:2])
            TS(DRV[:, D_C1:D_C1 + 8], DRV[:, 40:48], -8.0, None, ALU.mult, None, ["DRV"], ["DRV"])
            TT(DRV[:, D_PBS:D_PBS + 16], PRM[:, P_PB:P_PB + 16], PRM[:, P_PS:P_PS + 16], ALU.mult, ["PRM", "DRV"], ["DRV"])

        nblk = SEQ // 128
        stg4 = [(stage[:, 0, :], [("stage", 0)], "stg0"), (stage[:, 1, :], [("stage", 1)], "stg1"),
                (lnT[:, :, :].rearrange("p a c -> p (a c)"), [("lnT", 0), ("lnT", 1)], "stg2"),
                (sgb[:, :, :].rearrange("p a c -> p (a c)"), [("sg", 0), ("sg", 1)], "stg3")]
        for b in range(nblk):
            sk = b % 4
            sbuf_, skeys, ssem = stg4[sk]
            t = b // 4
            c = PAD + b * 128
            DMA("sp", sbuf_, xp[b * 128:(b + 1) * 128, :], (), skeys, ssem)
            for half in range(2):
                bank = 2 * sk + half
                for q in range(4):
                    kc = half * 4 + q
                    TR(ps[bank][:, q * 128:(q + 1) * 128], sbuf_[:, kc * 128:(kc + 1) * 128], ident[:],
                       skeys + ["identraw"], [("ps", bank)])
                src = ps[bank][:, :].rearrange("p (k c) -> p k c", k=4)
                ACOPY(xT[:, half * 4:half * 4 + 4, c:c + 128], src, [("ps", bank)], [("x", half * 4 + q, t) for q in range(4)])
                VCOPY(xb[:, half * 4:half * 4 + 4, c:c + 128], src, [("ps", bank)], [("xb", half * 4 + q, t) for q in range(4)])
        if 'sin' not in SKIP:
            DMA("sp", stage[0:NS, 0, :], xs.get(), (), [("stage", 0)], "stg0")
        for kc in range(KC if 'sin' not in SKIP else 0):
            TR(ps[0][:, kc * NS:(kc + 1) * NS], stage[0:NS, 0, kc * 128:(kc + 1) * 128], ident[0:NS, 0:NS],
               [("stage", 0), "identraw"], [("ps", 0)])
        srcs = ps[0][:, 0:KC * NS].rearrange("p (k c) -> p k c", k=KC)
        if 'sin' not in SKIP:
            ACOPY(xT[:, :, CS:CS + NS], srcs, [("ps", 0)], [("x", k, NT) for k in range(KC)])
            VCOPY(xb[:, :, CS:CS + NS], srcs, [("ps", 0)], [("xb", k, NT) for k in range(KC)])

        def resid_evac(t, m, ybank, mode):
            c0, W = tiles[t]
            xs_ = xT[:, m, c0:c0 + W]
            if mode == "first":
                STT(xs_, xs_, ALPHA, ps[ybank][:, :W], ALU.mult, ALU.add, [("ps", ybank), ("x", m, t)], [("x", m, t)])
            else:
                TT(xs_, xs_, ps[ybank][:, :W], ALU.add, [("ps", ybank), ("x", m, t)], [("x", m, t)])

        ln_par = [0]

        def ln_banks():
            return (6, 7) if ln_par[0] % 2 == 0 else (2, 3)

        def ln_prep(t, m, on_pool=False):
            c0, W = tiles[t]
            k = m % 2
            if on_pool and m % 2 == 1:
                ACOPY(xb[:, m, c0:c0 + W], xT[:, m, c0:c0 + W], [("x", m, t)], [("xb", m, t)])
            else:
                VCOPY(xb[:, m, c0:c0 + W], xT[:, m, c0:c0 + W], [("x", m, t)], [("xb", m, t)])
            ACT(zsq[:, k, :W], xT[:, m, c0:c0 + W], AF.Square, [("x", m, t)], [("zsq", k)])
            ln_drip(1)

        def ln_stats(t, m):
            c0, W = tiles[t]
            k = m % 2
            b6, b7 = ln_banks()
            MM(ps[b6][:, :W], onesb[:], xb[:, m, c0:c0 + W], m == 0, m == KC - 1, ["onesb", ("xb", m, t)], [("ps", b6)])
            MM(ps[b7][:, :W], onesb[:], zsq[:, k, :W], m == 0, m == KC - 1, ["onesb", ("zsq", k)], [("ps", b7)])

        ln_pending = []

        def ln_drip(n):
            for _ in range(n):
                if ln_pending:
                    ln_pending.pop(0)()

        def ln_flush():
            while ln_pending:
                ln_pending.pop(0)()

        def ln_finalize(t, l, idx, defer=False):
            c0, W = tiles[t]
            row = (l * 3 + idx) * 8
            b6, b7 = ln_banks()
            ln_par[0] += 1
            ln_flush()
            ACT(lnA[:, :W], ps[b6][:, :W], AF.Square, [("ps", b6)], ["lnA"])
            TT(lnA[:, :W], ps[b7][:, :W], lnA[:, :W], ALU.subtract, [("ps", b7), "lnA"], ["lnA"])
            ACT(lnB[:, :W], lnA[:, :W], AF.Ln, ["lnA", "cst"], ["lnB"], bias=cst[:, 0:1])
            ACT(lnB[:, :W], lnB[:, :W], AF.Exp, ["lnB"], ["lnB"], scale=-0.5)
            TT(lnC[:, :W], ps[b6][:, :W], lnB[:, :W], ALU.mult, [("ps", b6), "lnB"], ["lnC"])
            def piece(m):
                k = m % 2
                xs_ = xT[:, m, c0:c0 + W]
                TT(lnT[:, k, :W], xs_, lnB[:, :W], ALU.mult, [("x", m, t), "lnB"], [("lnT", k)])
                TT(lnT[:, k, :W], lnT[:, k, :W], lnC[:, :W], ALU.subtract, [("lnT", k), "lnC"], [("lnT", k)])
                g_ap = PRM[:, P_LNG + row + m:P_LNG + row + m + 1]
                b_ap = PRM[:, P_LNB + row + m:P_LNB + row + m + 1]
                ACT(xs_, lnT[:, k, :W], AF.Identity, [("lnT", k), "PRM"], [("x", m, t)], bias=b_ap, scale=g_ap)
                ACT(xb[:, m, c0:c0 + W], lnT[:, k, :W], AF.Identity, [("lnT", k), "PRM"], [("xb", m, t)], bias=b_ap, scale=g_ap)

            for m in range(KC):
                ln_pending.append(lambda m=m: piece(m))
            if not defer:
                ln_flush()

        aT = arena[:, 0:12 * TTP].rearrange("p (f c) -> p f c", f=12)
        cnt = {"gu": 0, "y": 0}

        def ffn(l, i, ln_idx):
            for hf, (f0, f1) in enumerate(HALVES):
                nf = f1 - f0
                for blk0 in range(f0, f1, 2):
                    nb = min(2, f1 - blk0)

                    def pairs(slotv, blk0=blk0, nb=nb):
                        gv = slotv[:, 0:2048].rearrange("p (k c) -> p k c", k=KC)[:, :, 0:nb * 128]
                        uv = slotv[:, 2048:4096].rearrange("p (k c) -> p k c", k=KC)[:, :, 0:nb * 128]
                        gs = wg[l, i].rearrange("(k p) c -> p k c", p=128)[:, :, blk0 * 128:(blk0 + nb) * 128]
                        us = wu[l, i].rearrange("(k p) c -> p k c", p=128)[:, :, blk0 * 128:(blk0 + nb) * 128]
                        return [(gv, gs), (uv, us)]

                    slot = ring_fill(pairs)
                    gview = ring[:, slot, 0:2048].rearrange("p (k c) -> p k c", k=KC)
                    uview = ring[:, slot, 2048:4096].rearrange("p (k c) -> p k c", k=KC)
                    for t, (c0, W) in enumerate(tiles):
                        for j in range(nb):
                            fl = blk0 + j - f0
                            kk = cnt["gu"] % 2
                            cnt["gu"] += 1
                            gb, ub = kk, 2 + kk
                            for kc in range(KC):
                                MM(ps[gb][:, :W], gview[:, kc, j * 128:(j + 1) * 128], xb[:, kc, c0:c0 + W], kc == 0, kc == KC - 1,
                                   [("w", slot), ("xb", kc, t)], [("ps", gb)])
                            for kc in range(KC):
                                MM(ps[ub][:, :W], uview[:, kc, j * 128:(j + 1) * 128], xb[:, kc, c0:c0 + W], kc == 0, kc == KC - 1,
                                   [("w", slot), ("xb", kc, t)], [("ps", ub)])
                            ACT(sgb[:, kk, :W], ps[gb][:, :W], AF.Silu, [("ps", gb)], [("sg", kk)])
                            STT(aT[:, fl, c0:c0 + W], sgb[:, kk, :W], 0.5, ps[ub][:, :W], ALU.mult, ALU.mult, [("sg", kk), ("ps", ub)], [("A", "aT", fl, t)])
                slots = []
                for g0 in range(0, nf, 4):
                    ng = min(4, nf - g0)

                    def pairs(slotv, g0=g0, ng=ng):
                        dv = slotv[:, 0:ng * 1024].rearrange("p (f c) -> p f c", f=ng)
                        src = wd[l, i, (f0 + g0) * 128:(f0 + g0 + ng) * 128, :].rearrange("(f p) c -> p f c", p=128)
                        return [(dv, src)]

                    slots.append(ring_fill(pairs))
                for t, (c0, W) in enumerate(tiles):
                    for m in range(KC):
                        yb = (4, 5, 0, 1)[cnt["y"] % 4]
                        cnt["y"] += 1
                        for fi in range(nf):
                            slot = slots[fi // 4]
                            wv = ring[:, slot, (fi % 4) * 1024 + m * 128:(fi % 4) * 1024 + (m + 1) * 128]
                            MM(ps[yb][:, :W], wv, aT[:, fi, c0:c0 + W], fi == 0, fi == nf - 1,
                               [("w", slot), ("A", "aT", fi, t)], [("ps", yb)])
                        resid_evac(t, m, yb, "first" if hf == 0 else "acc")
                        if hf == 1:
                            ln_prep(t, m, on_pool=True)
                            if m > 0:
                                ln_stats(t, m - 1)
                    if hf == 1:
                        ln_stats(t, KC - 1)
                        ln_finalize(t, l, ln_idx, defer=True)
                if hf == 1:
                    ln_flush()


        misc = sb("misc", [128, 32], F32)
        carve_off = [0]

        def carve_reset(off=0):
            carve_off[0] = off

        def carve(ncols, dtype=BF16):
            n16 = ncols * (2 if dtype == F32 else 1)
            n16 = (n16 + 15) // 16 * 16
            o = carve_off[0]
            assert o + n16 <= ARENA_COLS, ("arena overflow", o, n16, ARENA_COLS)
            carve_off[0] = o + n16
            v = arena[:, o:o + n16]
            if dtype == F32:
                v = v.bitcast(F32)
            return v[:, 0:ncols]

        def barrier():
            S.add("dve", lambda e: e.memset(misc[:, 31:32], 0.0), (), ["EPOCH"])

        def pool_mixer(l):
            j = l // 2
            barrier()
            carve_reset()
            dlt = [carve(8 * 512).rearrange("p (k c) -> p k c", k=8) for _ in range(2)]
            pq = [carve(2 * 528, F32).rearrange("p (a c) -> p a c", a=2) for _ in range(2)]
            ostg = carve(1024, F32)
            dls = carve(8 * NS).rearrange("p (k c) -> p k c", k=8)
            sums = carve(8 * NS, F32).rearrange("p (k c) -> p k c", k=8)
            selp = carve(2 * 4 * NS, F32).rearrange("p (h g c) -> p h g c", h=2, g=4)
            icnt = carve(64, F32).rearrange("p (g c) -> p g c", g=4)
            t16 = carve(16, F32)

            def pairs(slotv):
                dv = slotv[:, 0:2048].rearrange("p (g c) -> p g c", g=8)
                src = pool_w[j].rearrange("g (k p) n -> p (g k) n", p=128)
                return [(dv, src)]

            slot = ring_fill(pairs)
            pwv = ring[:, slot, 0:2048].rearrange("p (g c) -> p g c", g=8)
            DMAS("sp", [(selp[0:120], selp_d.get()), (icnt, icnt_d.get())], (), [("A", "selp")], "pc%d" % j)
            prow = P_PS + j * 8
            WIN = (2, 4, 8, 16)

            def mm_resid_ln(t, dl):
                c0, W = tiles[t]
                for m in range(KC):
                    g, mm_ = m // 2, m % 2
                    yb = 4 + cnt["y"] % 2
                    cnt["y"] += 1
                    for k in range(2):
                        MM(ps[yb][:, :W], pwv[:, g * 2 + k, mm_ * 128:(mm_ + 1) * 128], dl[:, 2 * g + k, :W], k == 0, k == 1,
                           [("w", slot), ("A", "dlt", id(dl))], [("ps", yb)])
                    xs_ = xT[:, m, c0:c0 + W]
                    ACT(xs_, xs_, AF.Identity, [("x", m, t), "DRV"], [("x", m, t)], bias=DRV[:, D_PBS + j * 8 + m:D_PBS + j * 8 + m + 1], scale=ALPHA)
                    STT(xs_, ps[yb][:, :W], PRM[:, prow + m:prow + m + 1], xs_, ALU.mult, ALU.add, [("ps", yb), ("x", m, t), "PRM"], [("x", m, t)])
                    ln_prep(t, m)
                    if m > 0:
                        ln_stats(t, m - 1)
                ln_stats(t, KC - 1)
                ln_finalize(t, l, 1, defer=True)

            def tok_rows_out(csrc, n, dst_ap_fn, key):
                for half in range(2):
                    for q in range(4):
                        kc = half * 4 + q
                        TR(ps[half][0:16, q * 128:(q + 1) * 128], xT[:, kc, csrc:csrc + 16], ident[:], [("x", kc, n), "identraw"], [("ps", half)])
                    ACOPY(ostg[0:16, half * 512:(half + 1) * 512], ps[half][0:16, :], [("ps", half)], [("A", "ostg")])
                o_ap, i_ap = dst_ap_fn(ostg)
                DMA("sp", o_ap, i_ap, [("A", "ostg")], (), key)

            DMAS("sp", [(stage[0:120, h, :], st_pool[j].rearrange("b i c -> (b i) c")[h * 120:(h + 1) * 120, :]) for h in range(2)],
                 (), [("stage", 0), ("stage", 1)], "stgP")
            for kc in range(KC):
                g = kc // 2
                for h in range(2):
                    MM(ps[2][:, kc * NS:(kc + 1) * NS], stage[0:120, h, kc * 128:(kc + 1) * 128], selp[0:120, h, g, :], h == 0, h == 1,
                       [("stage", 0), ("stage", 1), ("A", "selp")], [("ps", 2)])
            xs3 = xT[:, :, CS:CS + NS]
            TT(sums, xs3, ps[2][:, 0:KC * NS].rearrange("p (k c) -> p k c", k=KC), ALU.add, [("ps", 2)] + [("x", k, NT) for k in range(KC)], [("A", "sums")])
            for g in range(4):
                STT(dls[:, 2 * g:2 * g + 2, :], sums[:, 2 * g:2 * g + 2, :], 1.0 / WIN[g], xT[:, 2 * g:2 * g + 2, CS:CS + NS], ALU.mult, ALU.subtract,
                    [("A", "sums"), ("x", 2 * g, NT), ("x", 2 * g + 1, NT)], [("A", "dlt", id(dls))])
            DMA("sp", pool_s[j][:, 0:14, :], st_pool[j][:, 1:15, :], (), (), "pcp%d" % j)
            tok_rows_out(CS, NT, lambda o: (pool_s[j][:, 14, :], o[0:16, :]), "ostg")
            mm_resid_ln(NT, dls)

            def windows(t):
                c0, W = tiles[t]
                dl = dlt[t % 2]
                for g in range(4):
                    w = WIN[g]
                    k0 = 2 * g
                    rd = [("x", kc, tt) for kc in (k0, k0 + 1) for tt in ((t, t - 1) if t > 0 else (t,))]
                    L = W + w - 2
                    TT(pq[0][:, :, 0:L], xT[:, k0:k0 + 2, c0 - (w - 2):c0 + W], xT[:, k0:k0 + 2, c0 - (w - 1):c0 + W - 1], ALU.add, rd, [("A", "pq", 0)])
                    cur, sh = 0, 2
                    while sh < w:
                        L2 = L - sh
                        TT(pq[1 - cur][:, :, 0:L2], pq[cur][:, :, sh:sh + L2], pq[cur][:, :, 0:L2], ALU.add, [("A", "pq", cur)], [("A", "pq", 1 - cur)])
                        cur, L, sh = 1 - cur, L2, sh * 2
                    STT(dl[:, k0:k0 + 2, :W], pq[cur][:, :, 0:W], 1.0 / w, xT[:, k0:k0 + 2, c0:c0 + W], ALU.mult, ALU.subtract,
                        [("A", "pq", cur), ("x", k0, t), ("x", k0 + 1, t)], [("A", "dlt", id(dl))])
                    if t == 0:
                        for a in range(2):
                            kc = k0 + a
                            TT(t16[:, :], pq[cur][:, a, 0:16], icnt[:, g, :], ALU.mult, [("A", "pq", cur), ("A", "selp")], [("A", "t16")])
                            TT(dl[:, kc, 0:16], t16[:, :], xT[:, kc, c0:c0 + 16], ALU.subtract, [("A", "t16"), ("x", kc, t)], [("A", "dlt", id(dl))])

            windows(0)
            for t in range(NT):
                if t + 1 < NT:
                    windows(t + 1)
                else:
                    cl = PAD + SEQ - 16
                    tok_rows_out(cl, NT - 1, lambda o: (pool_p[j], o[1:16, :]), "ostg")
                mm_resid_ln(t, dlt[t % 2])
            ln_flush()

        LG = [float(np.log1p(-np.exp2(np.float32(-5.0 - h))).astype(np.float32)) for h in range(4)]
        CDEC = [float(np.exp(np.float32(LG[h]) * np.float32(128.0))) for h in range(4)]
        GAM = [float(np.exp(np.float32(LG[h]))) for h in range(4)]
        QSC = float(128 ** -0.5)

        def slot_view(slot):
            return ring[:, slot, :].rearrange("p (k c) -> p k c", k=KC)

        def fill_cols(j, base):
            def pairs(slotv):
                return [(slotv.rearrange("p (k c) -> p k c", k=KC), w_in[j].rearrange("(k p) c -> p k c", p=128)[:, :, base:base + 512])]
            return ring_fill(pairs)

        def fill_cols_swapped(j, base):
            def pairs(slotv):
                return [(slotv.rearrange("p (k c) -> p k c", k=KC), w_insw[j].rearrange("(k p) c -> p k c", p=128)[:, :, base:base + 512])]
            return ring_fill(pairs)

        def proj(slot, col0, bank, t, ncols=128):
            c0, W = tiles[t]
            sv = slot_view(slot)
            for kc in range(KC):
                MM(ps[bank][0:ncols, :W], sv[:, kc, col0:col0 + ncols], xb[:, kc, c0:c0 + W], kc == 0, kc == KC - 1,
                   [("w", slot), ("xb", kc, t)], [("ps", bank)])

        def mixer_a(l):
            j = l // 2
            barrier()
            carve_reset()
            Sst = carve(512, F32)
            Sbf = carve(512)
            decayT = carve(512, F32)
            qdec = carve(512, F32)
            wab = carve(512)
            wib = carve(512)
            o128 = carve(128)
            sgT = carve(4 * 512).rearrange("p (h c) -> p h c", h=4)
            mixT = carve(8 * 512).rearrange("p (k c) -> p k c", k=8)
            base = carve_off[0]
            hprev = misc[:, 0:4]
            uxprev = misc[:, 4:16].rearrange("p (n i) -> p n i", n=4)
            kdecs = misc[:, 16:20]
            AK = lambda *k: ("A",) + k

            DMAS("sp", [(decayT, decay_d.get()), (qdec, qdec_d.get()), (kdecs, kdecs_d.get())], (), [AK("dtab"), "kdecs"], "ma%d" % j)
            MEMSET(Sst, 0.0, [AK("Sst")])
            MEMSET(Sbf, 0.0, [AK("Sbf")])
            MEMSET(o128, 1.0 / 128, [AK("o128")])
            MEMSET(misc[:, 0:16], 0.0, ["hprev", "uxprev"])

            def interleave(*gens):
                gens = list(gens)
                while gens:
                    for g in list(gens):
                        try:
                            next(g)
                        except StopIteration:
                            gens.remove(g)

            def gn_gen(h, W, bs, banks):
                (t1, k1), (t2, k2), (tmp, kt), (ob, kob), (osq, ksq) = bs["t1"], bs["t2"], bs["gt"], bs["ob"], bs["osq"]
                bm, bq = banks
                VCOPY(ob[:, :W], ps[h][:, :W], [("ps", h)], kob)
                yield
                ACT(osq[:, :W], ps[h][:, :W], AF.Square, [("ps", h)], ksq)
                yield
                MM(ps[bm][:, :W], o128, ob[:, :W], True, True, [AK("o128")] + kob, [("ps", bm)])
                MM(ps[bq][:, :W], o128, osq[:, :W], True, True, [AK("o128")] + ksq, [("ps", bq)])
                yield
                ACT(t1[:, :W], ps[bm][:, :W], AF.Square, [("ps", bm)], k1)
                yield
                TT(t1[:, :W], ps[bq][:, :W], t1[:, :W], ALU.subtract, [("ps", bq)] + k1, k1)
                yield
                ACT(t1[:, :W], t1[:, :W], AF.Ln, k1 + ["cst"], k1, bias=cst[:, 0:1])
                yield
                ACT(t1[:, :W], t1[:, :W], AF.Exp, k1, k1, scale=-0.5)
                yield
                TT(t2[:, :W], ps[bm][:, :W], t1[:, :W], ALU.mult, [("ps", bm)] + k1, k2)
                yield
                TT(tmp[:, :W], ps[h][:, :W], t1[:, :W], ALU.mult, [("ps", h)] + k1, kt)
                yield
                TT(tmp[:, :W], tmp[:, :W], t2[:, :W], ALU.subtract, kt + k2, kt)
                yield
                STT(mixT[:, h, :W], tmp[:, :W], PRM[:, P_GNG + j * 4 + h:P_GNG + j * 4 + h + 1], sgT[:, h, :W], ALU.mult, ALU.mult,
                    kt + ["PRM", AK("sgT", h)], [AK("mixT", h)])
                yield

            def rotary_pair(slot_a, slot_b, t, t1, t2, cosb, sinb, sink):
                c0, W = tiles[t]
                for h in range(4):
                    ba, bb = (0, 1) if h % 2 == 0 else (2, 3)
                    proj(slot_a, h * 128, ba, t)
                    proj(slot_b, h * 128, bb, t)
                    TT(t1[:, :W], ps[ba][:, :W], cosb[:, :W], ALU.mult, [("ps", ba), AK("cosb")], [AK("t1")])
                    TT(t2[:, :W], ps[bb][:, :W], sinb[:, :W], ALU.mult, [("ps", bb), AK("sinb")], [AK("t2")])
                    sink(h)
                    ln_drip(1)

            def retention_and_gate(t):
                c0, W = tiles[t]
                sample = (t == NT)
                nch = W // 128
                barrier()
                carve_reset(base)
                qT_f, qdT_f, kT_f = carve(4 * W), carve(4 * W), carve(4 * W)
                vtok_f = carve(max(nch, 1) * 512)
                qT = qT_f.rearrange("p (h c) -> p h c", h=4)
                qdT = qdT_f.rearrange("p (h c) -> p h c", h=4)
                kT = kT_f.rearrange("p (h c) -> p h c", h=4)
                vtok = vtok_f.rearrange("p (k c) -> p k c", k=max(nch, 1))
                t1 = carve(W, F32)
                t2 = carve(W, F32)
                cosb = carve(W, F32)
                sinb = carve(W, F32)
                PT = carve(512)
                kd = carve(512)
                ob = carve(512)
                tc0 = (c0 - PAD)
                DMAS("sp", [(cosb[:, :W], costab[:, tc0:tc0 + W]), (sinb[:, :W], sintab[:, tc0:tc0 + W])], (), [AK("cosb"), AK("sinb")], "cs")

                sq_, sqs = fill_cols(j, 0), fill_cols_swapped(j, 0)

                def qsink(h):
                    TT(t1[:, :W], t1[:, :W], t2[:, :W], ALU.add, [AK("t1"), AK("t2")], [AK("t1")])
                    if sample:
                        S.act(lambda e: e.mul(qT[:, h, :W], t1[:, :W], QSC), [AK("t1")], [AK("qT", h)])
                    else:
                        ACOPY(qT[:, h, :W], t1[:, :W], [AK("t1")], [AK("qT", h)])
                        TT(qdT[:, h, :W].rearrange("p (c i) -> p c i", i=128), t1[:, :W].rearrange("p (c i) -> p c i", i=128),
                           qdec[:, h * 128:(h + 1) * 128].unsqueeze(1).broadcast_to([128, nch, 128]), ALU.mult,
                           [AK("t1"), AK("dtab")], [AK("qdT", h)])

                rotary_pair(sq_, sqs, t, t1, t2, cosb, sinb, qsink)
                sk_, sks = fill_cols(j, 512), fill_cols_swapped(j, 512)

                def ksink(h):
                    TT(kT[:, h, :W], t1[:, :W], t2[:, :W], ALU.add, [AK("t1"), AK("t2")], [AK("kT", h)])

                rotary_pair(sk_, sks, t, t1, t2, cosb, sinb, ksink)
                sg_ = fill_cols(j, 1536)
                for h in range(4):
                    gb = 4 + h % 2
                    proj(sg_, h * 128, gb, t)
                    ACT(sgT[:, h, :W], ps[gb][:, :W], AF.Silu, [("ps", gb)], [AK("sgT", h)])
                sv_ = fill_cols(j, 1024)
                svv = slot_view(sv_)
                if not sample:
                    for c in range(nch):
                        vb = 6 + c % 2
                        for kc in range(KC):
                            MM(ps[vb][:, :], xb[:, kc, c0 + c * 128:c0 + (c + 1) * 128], svv[:, kc, :], kc == 0, kc == KC - 1,
                               [("w", sv_), ("xb", kc, t)], [("ps", vb)])
                        ACOPY(vtok[:, c, :], ps[vb][:, :], [("ps", vb)], [AK("vtok", c)])
                    for c in range(nch):
                        cc = slice(c * 128, (c + 1) * 128)
                        for h in range(4):
                            hs = slice(h * 128, (h + 1) * 128)
                            MM(ps[4][:, hs], kT[:, h, cc], qT[:, h, cc], True, True, [AK("kT", h), AK("qT", h)], [("ps", 4)])
                        TT(PT[:, :], ps[4][:, :], decayT[:, :], ALU.mult, [("ps", 4), AK("dtab")], [AK("PT")])
                        for h in range(4):
                            hs = slice(h * 128, (h + 1) * 128)
                            MM(ps[h][:, cc], vtok[:, c, hs], PT[:, hs], True, False, [AK("vtok", c), AK("PT")], [("ps", h)])
                            MM(ps[h][:, cc], Sbf[:, hs], qdT[:, h, cc], False, True, [AK("Sbf"), AK("qdT", h)], [("ps", h)])
                        p5 = ps[5][:, :].bitcast(BF16)
                        for h in range(4):
                            hs = slice(h * 128, (h + 1) * 128)
                            TR(p5[:, hs], kT[:, h, cc], identb[:], [AK("kT", h), "identb"], [("ps", 5)])
                        for h in range(4):
                            hs = slice(h * 128, (h + 1) * 128)
                            TS(kd[:, hs], p5[:, hs], kdecs[:, h:h + 1], None, ALU.mult, None, [("ps", 5), "kdecs"], [AK("kd")])
                        for h in range(4):
                            hs = slice(h * 128, (h + 1) * 128)
                            MM(ps[6][:, hs], kd[:, hs], vtok[:, c, hs], True, True, [AK("kd"), AK("vtok", c)], [("ps", 6)])
                        for h in range(4):
                            hs = slice(h * 128, (h + 1) * 128)
                            STT(Sst[:, hs], Sst[:, hs], CDEC[h], ps[6][:, hs], ALU.mult, ALU.add, [AK("Sst"), ("ps", 6)], [AK("Sst")])
                        VCOPY(Sbf[:, :], Sst[:, :], [AK("Sst")], [AK("Sbf")])
                    if t == NT - 1:
                        DMA("sp", ret_p[j].rearrange("h d v -> d h v"), Sst.rearrange("p (h v) -> p h v", h=4), [AK("Sst")], (), "rp%d" % j)
                else:
                    ktok = carve(512)
                    km = [carve(512) for _ in range(2)]
                    GB = 2
                    NSB = 3
                    S4 = [carve(GB * 512, F32).rearrange("p (b c) -> p b c", b=GB) for _ in range(NSB)]
                    Sb4 = [carve(GB * 512).rearrange("p (b c) -> p b c", b=GB) for _ in range(NSB)]
                    for kc in range(KC):
                        MM(ps[6][0:NS, :], xb[:, kc, c0:c0 + NS], svv[:, kc, :], kc == 0, kc == KC - 1, [("w", sv_), ("xb", kc, t)], [("ps", 6)])
                    ACOPY(vtok[0:NS, 0, :], ps[6][0:NS, :], [("ps", 6)], [AK("vtok", 0)])
                    p5 = ps[5][:, :].bitcast(BF16)
                    for h in range(4):
                        hs = slice(h * 128, (h + 1) * 128)
                        TR(p5[0:NS, hs], kT[:, h, 0:NS], identb[:], [AK("kT", h), "identb"], [("ps", 5)])
                    ACOPY(ktok[0:NS, :], p5[0:NS, 0:512], [("ps", 5)], [AK("ktok")])
                    def s4_load(g):
                        sbx = g % NSB
                        DMA("sp", S4[sbx].rearrange("p b (h v) -> p b h v", h=4), st_ret[j][g * GB:(g + 1) * GB].rearrange("b h d v -> d b h v"),
                            (), [AK("S4", sbx)], "s4_%d" % sbx)

                    for g in range(NSB):
                        s4_load(g)
                    for g4 in range(NS // GB):
                        sb_ = g4 % NSB
                        for bb in range(GB):
                            b = g4 * GB + bb
                            kb = b % 2
                            TS(km[kb][0:NS, :], ktok[0:NS, :], ident[0:NS, b:b + 1], None, ALU.mult, None, [AK("ktok"), "identraw"], [AK("km", kb)])
                            for h in range(4):
                                hs = slice(h * 128, (h + 1) * 128)
                                MM(ps[4][:, hs], km[kb][0:NS, hs], vtok[0:NS, 0, hs], True, True, [AK("km", kb), AK("vtok", 0)], [("ps", 4)])
                            for h in range(4):
                                hs = slice(h * 128, (h + 1) * 128)
                                STT(S4[sb_][:, bb, hs], S4[sb_][:, bb, hs], GAM[h], ps[4][:, hs], ALU.mult, ALU.add, [AK("S4", sb_), ("ps", 4)], [AK("S4", sb_)])
                            ACOPY(Sb4[sb_][:, bb, :], S4[sb_][:, bb, :], [AK("S4", sb_)], [AK("Sb4", sb_)])
                            for h in range(4):
                                hs = slice(h * 128, (h + 1) * 128)
                                MM(ps[h][:, b:b + 1], Sb4[sb_][:, bb, hs], qT[:, h, b:b + 1], True, True, [AK("Sb4", sb_), AK("qT", h)], [("ps", h)])
                        DMA("sp", ret_s[j][g4 * GB:(g4 + 1) * GB].rearrange("b h d v -> d b h v"), S4[sb_].rearrange("p b (h v) -> p b h v", h=4),
                            [AK("S4", sb_)], (), "s4_%d" % sb_)
                        if g4 + NSB < NS // GB:
                            s4_load(g4 + NSB)
                set0 = dict(t1=(t1, [AK("t1")]), t2=(t2, [AK("t2")]), gt=(cosb, [AK("cosb")]), ob=(ob, [AK("ob")]), osq=(PT, [AK("PT")]))
                if sample:
                    for h in range(4):
                        for _ in gn_gen(h, W, set0, (7, 5)):
                            pass
                else:
                    half = 2 * W
                    set1 = dict(t1=(qT_f[:, 0:half].bitcast(F32), [AK("qT", 0), AK("qT", 1)]),
                                t2=(qT_f[:, half:2 * half].bitcast(F32), [AK("qT", 2), AK("qT", 3)]),
                                gt=(qdT_f[:, 0:half].bitcast(F32), [AK("qdT", 0), AK("qdT", 1)]),
                                ob=(qdT_f[:, half:half + W], [AK("qdT", 2)]), osq=(qdT_f[:, half + W:half + 2 * W], [AK("qdT", 3)]))
                    set2 = dict(t1=(kT_f[:, 0:half].bitcast(F32), [AK("kT", 0), AK("kT", 1)]),
                                t2=(kT_f[:, half:2 * half].bitcast(F32), [AK("kT", 2), AK("kT", 3)]),
                                gt=(vtok_f[:, 0:half].bitcast(F32), [AK("vtok", 0), AK("vtok", 1)]),
                                ob=(vtok_f[:, half:half + W], [AK("vtok", 2)]), osq=(vtok_f[:, half + W:half + 2 * W], [AK("vtok", 3)]))
                    interleave(gn_gen(0, W, set1, (7, 5)), gn_gen(1, W, set2, (6, 4)))
                    interleave(gn_gen(2, W, set0, (7, 5)), gn_gen(3, W, set1, (6, 4)))

            def lru(t):
                c0, W = tiles[t]
                sample = (t == NT)
                barrier()
                carve_reset(base)
                NUX = 4 if sample else 2
                uxe = carve(NUX * (W + 8), F32).rearrange("p (n c) -> p n c", n=NUX)
                uc2 = [carve(W, F32) for _ in range(2)]
                ucb2 = [carve(W) for _ in range(2)]
                aa2 = [carve(W, F32) for _ in range(2)]
                ei2 = [carve(W, F32) for _ in range(2)]
                a2 = carve(W, F32)
                iu = carve(W, F32)
                hh = carve(W, F32)
                sq = carve(W, F32)
                if sample:
                    cbuf = carve(4 * 48, F32).rearrange("p (n b i) -> p n b i", n=4, i=3)
                    h0T = carve(4 * NS, F32).rearrange("p (n b) -> p n b", n=4)
                    hsT = carve(4 * NS, F32).rearrange("p (n b) -> p n b", n=4)
                    tokb = carve(512, F32)
                    DMAS("sp", [(stage[0:48, 0, 0:512], st_conv[j].rearrange("b i c -> (b i) c")), (stage[0:NS, 1, 0:512], st_h[j])],
                         (), [("stage", 0), ("stage", 1)], "stgL")
                    for n in range(4):
                        TR(ps[6][:, n * 48:(n + 1) * 48], stage[0:48, 0, n * 128:(n + 1) * 128], ident[0:48, 0:48], [("stage", 0), "identraw"], [("ps", 6)])
                        TR(ps[7][:, n * NS:(n + 1) * NS], stage[0:NS, 1, n * 128:(n + 1) * 128], ident[0:NS, 0:NS], [("stage", 1), "identraw"], [("ps", 7)])
                    ACOPY(cbuf.rearrange("p n b i -> p (n b i)"), ps[6][:, 0:192], [("ps", 6)], [AK("cbuf")])
                    ACOPY(h0T.rearrange("p n b -> p (n b)"), ps[7][:, 0:4 * NS], [("ps", 7)], [AK("h0T")])
                    DMA("sp", conv_s[j][:, 0:2, :], st_conv[j][:, 1:3, :], (), (), "ccp%d" % j)
                if t == 0:
                    DMAS("pool", [(wab.rearrange("p (n d) -> p n d", n=4), lru_wa[j].rearrange("n c d -> c n d")),
                                  (wib.rearrange("p (n d) -> p n d", n=4), lru_wi[j].rearrange("n c d -> c n d"))], (), [AK("wab")], "mw%d" % j)
                sux, sug = fill_cols(j, 2048), fill_cols(j, 2560)

                def early(n):
                    p = n % 2
                    xi = n % NUX
                    uc, ucb, aa, ei = uc2[p], ucb2[p], aa2[p], ei2[p]
                    ns = slice(n * 128, (n + 1) * 128)
                    ub = n % 2
                    proj(sux, n * 128, ub, t)
                    yield
                    ACOPY(uxe[:, xi, 3:3 + W], ps[ub][:, :W], [("ps", ub)], [AK("uxe", xi)])
                    w = lambda i: PRM[:, P_CW + (j * 4 + i) * 4 + n:P_CW + (j * 4 + i) * 4 + n + 1]
                    cbias = PRM[:, P_CB + j * 4 + n:P_CB + j * 4 + n + 1]
                    yield
                    TS(uc[:, :W], uxe[:, xi, 3:3 + W], w(3), cbias, ALU.mult, ALU.add, [AK("uxe", xi), "PRM"], [AK("uc", p)])
                    if not sample:
                        ACOPY(uxe[:, xi, 0:3], uxprev[:, n, :], ["uxprev"], [AK("uxe", xi)])
                        yield
                        for i in (2, 1, 0):
                            STT(uc[:, :W], uxe[:, xi, i:i + W], w(i), uc[:, :W], ALU.mult, ALU.add, [AK("uxe", xi), AK("uc", p), "PRM"], [AK("uc", p)])
                            yield
                        ACOPY(uxprev[:, n, :], uxe[:, xi, W:W + 3], [AK("uxe", xi)], ["uxprev"])
                    else:
                        for i in (2, 1, 0):
                            STT(uc[:, :W], cbuf[:, n, :, i], w(i), uc[:, :W], ALU.mult, ALU.add, [AK("cbuf"), AK("uc", p), "PRM"], [AK("uc", p)])
                            yield
                    VCOPY(ucb[:, :W], uc[:, :W], [AK("uc", p)], [AK("ucb", p)])
                    yield
                    MM(ps[2][:, :W], wab[:, ns], ucb[:, :W], True, True, [AK("wab"), AK("ucb", p)], [("ps", 2)])
                    MM(ps[3][:, :W], wib[:, ns], ucb[:, :W], True, True, [AK("wab"), AK("ucb", p)], [("ps", 3)])
                    yield
                    nba = DRV[:, D_NBA + j * 4 + n:D_NBA + j * 4 + n + 1]
                    nbi = DRV[:, D_NBI + j * 4 + n:D_NBI + j * 4 + n + 1]
                    c1 = DRV[:, D_C1 + j * 4 + n:D_C1 + j * 4 + n + 1]
                    ACT(aa[:, :W], ps[2][:, :W], AF.Exp, [("ps", 2), "DRV"], [AK("aa", p)], bias=nba, scale=-1.0)
                    yield
                    ACT(ei[:, :W], ps[3][:, :W], AF.Exp, [("ps", 3), "DRV"], [AK("ei", p)], bias=nbi, scale=-1.0)
                    yield
                    ACT(aa[:, :W], aa[:, :W], AF.Ln, [AK("aa", p), "cst"], [AK("aa", p)], bias=cst[:, 1:2])
                    yield
                    ACT(ei[:, :W], ei[:, :W], AF.Ln, [AK("ei", p), "cst"], [AK("ei", p)], bias=cst[:, 1:2])
                    yield
                    ACT(aa[:, :W], aa[:, :W], AF.Exp, [AK("aa", p)], [AK("aa", p)], scale=-1.0)
                    yield
                    ACT(ei[:, :W], ei[:, :W], AF.Exp, [AK("ei", p)], [AK("ei", p)], scale=-1.0)
                    yield
                    ACT(aa[:, :W], aa[:, :W], AF.Exp, [AK("aa", p), "DRV"], [AK("aa", p)], scale=c1)
                    yield

                def late(n):
                    p = n % 2
                    uc, aa, ei = uc2[p], aa2[p], ei2[p]
                    ACT(a2[:, :W], aa[:, :W], AF.Square, [AK("aa", p)], [AK("a2")])
                    yield
                    gb = 4 + n % 2
                    proj(sug, n * 128, gb, t)
                    yield
                    ACT(a2[:, :W], a2[:, :W], AF.Ln, [AK("a2"), "cst"], [AK("a2")], bias=cst[:, 3:4], scale=-1.0)
                    yield
                    ACT(sq[:, :W], ps[gb][:, :W], AF.Square, [("ps", gb)], [AK("sq")])
                    yield
                    ACT(a2[:, :W], a2[:, :W], AF.Exp, [AK("a2")], [AK("a2")], scale=0.5)
                    TT(iu[:, :W], ei[:, :W], uc[:, :W], ALU.mult, [AK("ei", p), AK("uc", p)], [AK("iu")])
                    yield
                    TS(sq[:, :W], sq[:, :W], 0.044715, 1.0, ALU.mult, ALU.add, [AK("sq")], [AK("sq")])
                    yield
                    TT(iu[:, :W], iu[:, :W], a2[:, :W], ALU.mult, [AK("iu"), AK("a2")], [AK("iu")])
                    yield
                    TT(sq[:, :W], sq[:, :W], ps[gb][:, :W], ALU.mult, [AK("sq"), ("ps", gb)], [AK("sq")])
                    yield
                    if not sample:
                        S.dve(lambda e, n=n: e.tensor_tensor_scan(hh[:, :W], aa[:, :W], iu[:, :W], hprev[:, n:n + 1], ALU.mult, ALU.add),
                              [AK("aa", p), AK("iu"), "hprev"], [AK("hh")])
                        ACOPY(hprev[:, n:n + 1], hh[:, W - 1:W], [AK("hh")], ["hprev"])
                    else:
                        TT(hh[:, :W], aa[:, :W], h0T[:, n, :], ALU.mult, [AK("aa", p), AK("h0T")], [AK("hh")])
                        TT(hh[:, :W], hh[:, :W], iu[:, :W], ALU.add, [AK("hh"), AK("iu")], [AK("hh")])
                        ACOPY(hsT[:, n, :], hh[:, :W], [AK("hh")], [AK("hsT")])
                    yield
                    ACT(sq[:, :W], sq[:, :W], AF.Exp, [AK("sq")], [AK("sq")], scale=-2.0 * GELU_C)
                    yield
                    ACT(sq[:, :W], sq[:, :W], AF.Ln, [AK("sq"), "cst"], [AK("sq")], bias=cst[:, 1:2])
                    yield
                    ACT(sq[:, :W], sq[:, :W], AF.Exp, [AK("sq")], [AK("sq")], scale=-1.0)
                    yield
                    TT(sq[:, :W], sq[:, :W], ps[gb][:, :W], ALU.mult, [AK("sq"), ("ps", gb)], [AK("sq")])
                    yield
                    TT(mixT[:, 4 + n, :W], sq[:, :W], hh[:, :W], ALU.mult, [AK("sq"), AK("hh")], [AK("mixT", 4 + n)])
                    yield

                interleave(early(0))
                for n in range(4):
                    if n + 1 < 4:
                        interleave(late(n), early(n + 1))
                    else:
                        interleave(late(n))
                if sample:
                    for n in range(4):
                        TR(ps[6][0:NS, n * 128:(n + 1) * 128], hsT[:, n, :], ident[:], [AK("hsT"), "identraw"], [("ps", 6)])
                        TR(ps[7][0:NS, n * 128:(n + 1) * 128], uxe[:, n, 3:3 + NS], ident[:], [AK("uxe", n), "identraw"], [("ps", 7)])
                    ACOPY(tokb[0:NS, :], ps[6][0:NS, :], [("ps", 6)], [AK("tokb")])
                    DMA("sp", h_s[j], tokb[0:NS, :], [AK("tokb")], (), "tokb")
                    ACOPY(tokb[0:NS, :], ps[7][0:NS, :], [("ps", 7)], [AK("tokb")])
                    DMA("sp", conv_s[j][:, 2, :], tokb[0:NS, :], [AK("tokb")], (), "tokb")
                elif t == NT - 1:
                    tokp = carve(128, F32)
                    TR(ps[6][0:16, 0:128], misc[:, 0:16], ident[:], ["hprev", "uxprev", "identraw"], [("ps", 6)])
                    ACOPY(tokp[0:16, :], ps[6][0:16, 0:128], [("ps", 6)], [AK("tokp")])
                    prs = [(h_p[j].rearrange("(n p) -> n p", p=128), tokp[0:4, :])]
                    for n in range(4):
                        prs.append((conv_p[j][:, n * 128:(n + 1) * 128], tokp[4 + 3 * n:7 + 3 * n, :]))
                    DMAS("sp", prs, [AK("tokp")], (), "tokp%d" % j)

            def outproj_ln(t):
                c0, W = tiles[t]
                slots = []
                for half in range(2):
                    def pairs(slotv, half=half):
                        return [(slotv.rearrange("p (c m) -> p c m", c=4), w_out[j, half * 512:(half + 1) * 512, :].rearrange("(c p) m -> p c m", p=128))]
                    slots.append(ring_fill(pairs))
                for m in range(KC):
                    yb = 4 + cnt["y"] % 2
                    cnt["y"] += 1
                    for c in range(8):
                        slot = slots[c // 4]
                        wv = ring[:, slot, (c % 4) * 1024 + m * 128:(c % 4) * 1024 + (m + 1) * 128]
                        MM(ps[yb][:, :W], wv, mixT[:, c, :W], c == 0, c == 7, [("w", slot), AK("mixT", c)], [("ps", yb)])
                    resid_evac(t, m, yb, "first")
                    ln_prep(t, m)
                    if m > 0:
                        ln_stats(t, m - 1)
                ln_stats(t, KC - 1)
                ln_finalize(t, l, 1, defer=True)

            for t in range(NTL):
                ln_par[0] = 0
                retention_and_gate(t)
                lru(t)
                ln_par[0] = 0
                outproj_ln(t)
            ln_flush()

        for l in range(DEPTH):
            ffn(l, 0, 0)
            if l % 2 == 1 and cfg.mixC:
                pool_mixer(l)
            if l % 2 == 0 and cfg.mixA:
                mixer_a(l)
            if cfg.mixA or cfg.mixC:
                barrier()
            ffn(l, 1, 2)

        for b in range(nblk):
            sk = b % 4
            sbuf_, skeys, ssem = stg4[sk]
            t = b // 4
            c = PAD + b * 128
            for half in range(2):
                bank = 2 * sk + half
                for q in range(4):
                    kc = half * 4 + q
                    TR(ps[bank][:, q * 128:(q + 1) * 128], xT[:, kc, c:c + 128], ident[:], [("x", kc, t), "identraw"], [("ps", bank)])
                ACOPY(sbuf_[:, half * 512:(half + 1) * 512], ps[bank][:, :], [("ps", bank)], skeys)
            DMA("sp", y_p[b * 128:(b + 1) * 128, :], sbuf_, skeys, (), ssem)
        for half in range(2 if 'sout' not in SKIP else 0):
            for q in range(4):
                kc = half * 4 + q
                TR(ps[half][0:NS, q * 128:(q + 1) * 128], xT[:, kc, CS:CS + NS], ident[:], [("x", kc, NT), "identraw"], [("ps", half)])
            ACOPY(stage[0:NS, 0, half * 512:(half + 1) * 512], ps[half][0:NS, :], [("ps", half)], [("stage", 0)])
        if 'sout' not in SKIP:
            DMA("sp", y_s.get(), stage[0:NS, 0, :], [("stage", 0)], (), "stg0")

        S.finalize_and_emit()
    return nc


def host_tables(SEQ):
    f32 = np.float32
    half = 64
    inv = (f32(10000.0) ** (-(np.arange(half, dtype=f32)) / f32(half))).astype(f32)
    pos = np.concatenate([np.arange(SEQ, dtype=f32), np.full((NS,), PAST, f32)])
    ang = (pos[:, None] * inv[None, :]).astype(f32)
    cos = np.cos(ang).astype(f32).T
    sin = np.sin(ang).astype(f32).T
    costab = np.concatenate([cos, cos], axis=0)
    sintab = np.concatenate([-sin, sin], axis=0)
    lg = np.log1p(-np.exp2(-5.0 - np.arange(4, dtype=f32))).astype(f32)
    idx = np.arange(128, dtype=f32)
    sc = f32(128 ** -0.5)
    diff = idx[None, :] - idx[:, None]
    decayT = np.where(diff[:, None, :] >= 0, np.exp(lg[None, :, None] * np.maximum(diff[:, None, :], 0.0)), 0.0) * sc
    qdec = np.broadcast_to((np.exp(lg[:, None] * (idx[None, :] + 1.0)) * sc)[None], (128, 4, 128))
    kdec = np.broadcast_to(np.exp(lg[None, :, None] * (127.0 - idx[:, None, None])), (128, 4, 128))
    cdec = np.broadcast_to(np.exp(lg * 128.0)[None, :, None], (128, 4, 128))
    gam = np.broadcast_to(np.exp(lg)[None, :, None], (128, 4, 128))
    selp = np.zeros((240, 4, NS), f32)
    for g, w in enumerate((2, 4, 8, 16)):
        for b in range(NS):
            for i in range(15 - (w - 1), 15):
                selp[b * 15 + i, g, b] = 1.0
    selp = selp.reshape(2, 120, 4, NS).transpose(1, 0, 2, 3)
    icnt = np.zeros((128, 4, 16), f32)
    for g, w in enumerate((2, 4, 8, 16)):
        icnt[:, g, :] = 1.0 / np.minimum(float(w), np.arange(16, dtype=f32) + 1.0)
    kdecs = np.exp(lg[None, :] * (127.0 - idx[:, None])).astype(f32)
    c = lambda a: np.ascontiguousarray(a.reshape(128, -1).astype(f32))
    return dict(costab=np.ascontiguousarray(costab), sintab=np.ascontiguousarray(sintab), decayT=c(decayT), qdec=c(qdec),
                kdec=c(kdec), kdecs=np.ascontiguousarray(kdecs), cdec=c(cdec), gamtab=c(gam), selp=np.ascontiguousarray(selp.astype(f32)), icnt=icnt,
                ident=np.eye(128, dtype=f32))


def make_params(inp):
    f32 = np.float32
    rows = [inp["ln_g"].reshape(-1, 128), inp["ln_b"].reshape(-1, 128), inp["ret_gn_g"].reshape(-1, 128),
            inp["lru_conv_w"].reshape(-1, 128), inp["lru_conv_b"].reshape(-1, 128), inp["lru_ba"].reshape(-1, 128),
            inp["lru_bi"].reshape(-1, 128), inp["lru_lambda"].reshape(-1, 128), inp["pool_b"].reshape(-1, 128),
            inp["pool_scale"].reshape(-1, 128)]
    offs = [P_LNG, P_LNB, P_GNG, P_CW, P_CB, P_BA, P_BI, P_LAM, P_PB, P_PS]
    out = np.zeros((P_ROWS, 128), f32)
    for o, r in zip(offs, rows):
        out[o:o + r.shape[0]] = r
    return out


def make_in_maps(cfg, inp):
    SEQ = cfg.SEQ
    tabs = host_tables(SEQ)
    prm = make_params(inp)
    A = np.ascontiguousarray
    NWL = max(cfg.DEPTH, 1) if getattr(cfg, "tinyw", False) else 4
    shared = dict(wg=A(inp["w_ffn_gate"][:NWL]), wu=A(inp["w_ffn_up"][:NWL]), wd=A(inp["w_ffn_down"][:NWL]), w_in=A(inp["w_mix_in"]),
                  w_out=A(inp["w_mix_out"]), w_insw=A(inp["w_mix_in"][:, :, :1024].reshape(2, D, 8, 2, 64)[:, :, :, ::-1, :].reshape(2, D, 1024)), lru_wa=A(inp["lru_wa"]), lru_wi=A(inp["lru_wi"]), pool_w=A(inp["pool_w"]),
                  params=prm, **tabs)
    maps = []
    for c in range(8):
        sl = slice(c * NS, (c + 1) * NS)
        m = dict(shared)
        m["xp"] = A(inp["x_prompt"][c, :SEQ])
        m["xs"] = A(inp["x_sample"][sl, 0])
        m["st_ret"] = A(inp["state_ret"][:, sl])
        m["st_h"] = A(inp["state_lru_h"][:, sl])
        m["st_conv"] = A(inp["state_lru_conv"][:, sl])
        m["st_pool"] = A(inp["state_pool"][:, sl])
        maps.append(m)
    return maps


_CACHE = {}


def run(cfg, inp):
    key = (cfg.SEQ, cfg.DEPTH, cfg.mixA, cfg.mixC)
    if key not in _CACHE:
        _CACHE[key] = build_program(cfg)
    nc = _CACHE[key]
    maps = make_in_maps(cfg, inp)
    used = set()
    for alloc in nc.allocations:
        if isinstance(alloc, mybir.MemoryLocationSet) and alloc.kind == "ExternalInput":
            used.add(alloc.memorylocations[0].name)
    maps = [{k: v for k, v in m.items() if k in used} for m in maps]
    res = run_bass_kernel_spmd(nc, maps, core_ids=list(range(8)))
    R = res.results
    shp = dict(y_p=(cfg.SEQ, D), y_s=(NS, D), ret_p=(2, 4, 128, 128), h_p=(2, 512), conv_p=(2, 3, 512), pool_p=(2, 15, D),
               ret_s=(2, NS, 4, 128, 128), h_s=(2, NS, 512), conv_s=(2, NS, 3, 512), pool_s=(2, NS, 15, D))
    for r in R:
        for k, sh in shp.items():
            if k not in r:
                r[k] = np.zeros(sh, np.float32)
    cat = lambda k, ax: np.concatenate([r[k] for r in R], axis=ax)
    y_p = np.stack([r["y_p"] for r in R], 0)
    y_s = cat("y_s", 0)[:, None, :]
    ret_p = np.stack([r["ret_p"] for r in R], 1)
    h_p = np.stack([r["h_p"] for r in R], 1)
    conv_p = np.stack([r["conv_p"] for r in R], 1)
    pool_p = np.stack([r["pool_p"] for r in R], 1)
    ret_s = cat("ret_s", 1)
    h_s = cat("h_s", 1)
    conv_s = cat("conv_s", 1)
    pool_s = cat("pool_s", 1)
    return (y_p, y_s, ret_p, h_p, conv_p, pool_p, ret_s, h_s, conv_s, pool_s)


def kernel(**inputs):
    inp = {k: np.asarray(v) for k, v in inputs.items()}
    outs = run(Cfg(), inp)
    return tuple(np.ascontiguousarray(o.astype(np.float32)) for o in outs)
```
